# Optimizing a Trainium2 kernel written in Bass

```python
import jax, jax.numpy as jnp
from jax import lax
import numpy as np

D_MODEL = 2048
BATCH = 8
SEQ = 2048
DEPTH = 2

N_HEADS_TOTAL = 16
HEAD_DIM = D_MODEL // N_HEADS_TOTAL
DIL_PATTERNS = ((128, 1), (512, 4), (2048, 16))
N_DIL_GROUPS = len(DIL_PATTERNS)
HEADS_PER_GROUP = 4
A_HEADS = N_DIL_GROUPS * HEADS_PER_GROUP
A_OUT = HEADS_PER_GROUP * HEAD_DIM
SB_HEADS = 4
B_OUT = SB_HEADS * HEAD_DIM
A_W = A_HEADS * HEAD_DIM
IN_SPLITS = (A_W, A_W, A_W, B_OUT, B_OUT, B_OUT, D_MODEL, D_MODEL)
IN_WIDTH = sum(IN_SPLITS)
D_FF = -(-8 * D_MODEL // (3 * 256)) * 256
BLOCK = 128
ROPE_THETA = 10000.0
EPS = 1e-6

kernel_name = "hybrid_dilated_stickbreaking_adaln_block"


def rms_norm(x, g):
    xf = x.astype(jnp.float32)
    y = xf * lax.rsqrt(jnp.mean(xf * xf, axis=-1, keepdims=True) + EPS)
    return y * g.astype(jnp.float32)


def rope_tables(seq):
    inv = jnp.power(ROPE_THETA, -jnp.arange(0, HEAD_DIM, 2, dtype=jnp.float32) / HEAD_DIM)
    ang = jnp.arange(seq, dtype=jnp.float32)[:, None] * inv[None, :]
    return jnp.cos(ang), jnp.sin(ang)


def apply_rope(x, cos, sin):
    half = HEAD_DIM // 2
    x1, x2 = x[..., :half], x[..., half:]
    cs, sn = cos[None, :, None, :], sin[None, :, None, :]
    return jnp.concatenate([x1 * cs - x2 * sn, x2 * cs + x1 * sn], axis=-1)


def dilated_window_attention(q, k, v, window, dilation):
    B, S, H, hd = q.shape
    L = S // dilation
    w_sub = window // dilation
    Lp = -(-L // BLOCK) * BLOCK
    nb = Lp // BLOCK

    def to_sub(t):
        t = t.reshape(B, L, dilation, H, hd).transpose(0, 2, 3, 1, 4)
        t = jnp.pad(t, ((0, 0), (0, 0), (0, 0), (0, Lp - L), (0, 0)))
        return t.reshape(B, dilation, H, nb, BLOCK, hd)

    qb, kb, vb = to_sub(q), to_sub(k), to_sub(v)

    def with_prev(t):
        prev = jnp.pad(t, ((0, 0), (0, 0), (0, 0), (1, 0), (0, 0), (0, 0)))[:, :, :, :nb]
        return jnp.concatenate([prev, t], axis=4)

    kk, vv = with_prev(kb), with_prev(vb)
    s = jnp.einsum('brhnqd,brhnkd->brhnqk', qb, kk) * (hd ** -0.5)
    qi = jnp.arange(BLOCK)[:, None]
    kj = jnp.arange(2 * BLOCK)[None, :]
    dist = qi + BLOCK - kj
    band = (dist >= 0) & (dist <= w_sub)
    valid = (jnp.arange(nb)[:, None, None] * BLOCK + kj[None] - BLOCK) >= 0
    mask = band[None] & valid
    s = jnp.where(mask, s, -jnp.inf)
    m = jnp.max(s, axis=-1, keepdims=True)
    p = jnp.exp(s - m)
    den = jnp.sum(p, axis=-1)
    o = jnp.einsum('brhnqk,brhnkd->brhnqd', p, vv) / den[..., None]
    lse = m[..., 0] + jnp.log(den)
    o = o.reshape(B, dilation, H, Lp, hd)[:, :, :, :L].transpose(0, 3, 1, 2, 4).reshape(B, S, H, hd)
    lse = lse.reshape(B, dilation, H, Lp)[:, :, :, :L].transpose(0, 3, 1, 2).reshape(B, S, H)
    return o, lse


def mixer_a(qa, ka, va, qn_g, kn_g, cos, sin):
    B, S = qa.shape[0], qa.shape[1]
    q = apply_rope(rms_norm(qa, qn_g), cos, sin)
    k = apply_rope(rms_norm(ka, kn_g), cos, sin)
    v = va.astype(jnp.float32)
    outs, lses = [], []
    for g, (window, dilation) in enumerate(DIL_PATTERNS):
        lo, hi = g * HEADS_PER_GROUP, (g + 1) * HEADS_PER_GROUP
        o, lse = dilated_window_attention(q[:, :, lo:hi], k[:, :, lo:hi], v[:, :, lo:hi], window, dilation)
        outs.append(o)
        lses.append(lse)
    o = jnp.stack(outs, 0)
    w = jax.nn.softmax(jnp.stack(lses, 0), axis=0)
    return jnp.sum(w[..., None] * o, axis=0).reshape(B, S, A_OUT)


def stick_breaking_attention(q, k, v):
    B, S, H, hd = q.shape
    nb = S // BLOCK
    qh = q.astype(jnp.float32).transpose(0, 2, 1, 3)
    kh = k.astype(jnp.float32).transpose(0, 2, 1, 3)
    vh = v.astype(jnp.float32).transpose(0, 2, 1, 3)
    qblocks = qh.reshape(B, H, nb, BLOCK, hd).transpose(2, 0, 1, 3, 4)
    key_pos = jnp.arange(S)
    scale = hd ** -0.5

    def one_block(args):
        qblk, i = args
        z = jnp.einsum('bhqd,bhkd->bhqk', qblk, kh) * scale
        qpos = i * BLOCK + jnp.arange(BLOCK)
        causal = key_pos[None, :] < qpos[:, None]
        log_beta = jax.nn.log_sigmoid(z)
        log_1mb = jnp.where(causal, jax.nn.log_sigmoid(-z), 0.0)
        shifted = jnp.concatenate([log_1mb[..., 1:], jnp.zeros_like(log_1mb[..., :1])], axis=-1)
        after = lax.cumsum(shifted, axis=3, reverse=True)
        a = jnp.where(causal, jnp.exp(log_beta + after), 0.0)
        return jnp.einsum('bhqk,bhkd->bhqd', a, vh)

    o = lax.map(one_block, (qblocks, jnp.arange(nb)))
    return o.transpose(1, 0, 3, 2, 4).reshape(B, S, H * hd)


def setup_inputs(seed: int = 0) -> dict:
    key = jax.random.key(seed)
    ks = jax.random.split(key, 16)
    f32 = jnp.float32

    def w(k, shape, fan_in, mult=1.0):
        return jax.random.normal(k, shape, f32) * (mult * fan_in ** -0.5)

    return {
        "x": jax.random.normal(ks[0], (BATCH, SEQ, D_MODEL), f32),
        "c": jax.random.normal(ks[1], (BATCH, D_MODEL), f32),
        "w_ada": w(ks[2], (DEPTH, D_MODEL, 6 * D_MODEL), D_MODEL, 0.5),
        "b_ada": 0.01 * jax.random.normal(ks[3], (DEPTH, 6 * D_MODEL), f32),
        "norm1_g": 1.0 + 0.02 * jax.random.normal(ks[4], (DEPTH, D_MODEL), f32),
        "norm2_g": 1.0 + 0.02 * jax.random.normal(ks[5], (DEPTH, D_MODEL), f32),
        "w_in": w(ks[6], (DEPTH, D_MODEL, IN_WIDTH), D_MODEL),
        "qn_g": 1.0 + 0.02 * jax.random.normal(ks[7], (DEPTH, HEAD_DIM), f32),
        "kn_g": 1.0 + 0.02 * jax.random.normal(ks[8], (DEPTH, HEAD_DIM), f32),
        "w_branch_a": w(ks[9], (DEPTH, A_OUT, D_MODEL), A_OUT),
        "w_branch_b": w(ks[10], (DEPTH, B_OUT, D_MODEL), B_OUT),
        "w_out": w(ks[11], (DEPTH, D_MODEL, D_MODEL), D_MODEL),
        "w_gate_up": w(ks[12], (DEPTH, D_MODEL, 2 * D_FF), D_MODEL),
        "w_down": w(ks[13], (DEPTH, D_FF, D_MODEL), D_FF),
    }


def reference(x, c, w_ada, b_ada, norm1_g, norm2_g, w_in, qn_g, kn_g,
              w_branch_a, w_branch_b, w_out, w_gate_up, w_down):
    B, S, D = x.shape
    cos, sin = rope_tables(S)
    offs = [0]
    for n in IN_SPLITS:
        offs.append(offs[-1] + n)
    c_act = jax.nn.silu(c)
    h = x
    for l in range(DEPTH):
        mod = (c_act @ w_ada[l] + b_ada[l])[:, None, :]
        shift1 = mod[..., 0 * D:1 * D]
        scale1 = mod[..., 1 * D:2 * D]
        gate1 = mod[..., 2 * D:3 * D]
        shift2 = mod[..., 3 * D:4 * D]
        scale2 = mod[..., 4 * D:5 * D]
        gate2 = mod[..., 5 * D:6 * D]

        u = (rms_norm(h, norm1_g[l]) * (1.0 + scale1) + shift1).astype(h.dtype)
        proj = u @ w_in[l]
        parts = [proj[..., offs[i]:offs[i + 1]] for i in range(len(IN_SPLITS))]
        qa = parts[0].reshape(B, S, A_HEADS, HEAD_DIM)
        ka = parts[1].reshape(B, S, A_HEADS, HEAD_DIM)
        va = parts[2].reshape(B, S, A_HEADS, HEAD_DIM)
        qb = parts[3].reshape(B, S, SB_HEADS, HEAD_DIM)
        kb = parts[4].reshape(B, S, SB_HEADS, HEAD_DIM)
        vb = parts[5].reshape(B, S, SB_HEADS, HEAD_DIM)
        ga, gb = parts[6], parts[7]
        o_a = mixer_a(qa, ka, va, qn_g[l], kn_g[l], cos, sin).astype(h.dtype)
        o_b = stick_breaking_attention(qb, kb, vb).astype(h.dtype)
        y_a = o_a @ w_branch_a[l]
        y_b = o_b @ w_branch_b[l]
        merged = jax.nn.sigmoid(ga) * y_a + jax.nn.sigmoid(gb) * y_b
        h = h + gate1 * (merged @ w_out[l])

        u2 = (rms_norm(h, norm2_g[l]) * (1.0 + scale2) + shift2).astype(h.dtype)
        gu = u2 @ w_gate_up[l]
        h = h + gate2 * ((jax.nn.silu(gu[..., :D_FF]) * gu[..., D_FF:]) @ w_down[l])
    return h
```

```python
import contextlib
import numpy as np
import ml_dtypes
import concourse.bass as bass
import concourse.mybir as mybir
from concourse.bass_utils import run_bass_kernel_spmd

F32 = mybir.dt.float32
BF16 = mybir.dt.bfloat16
U8 = mybir.dt.uint8
AF = mybir.ActivationFunctionType
ALU = mybir.AluOpType
AX = mybir.AxisListType

S = 2048
D = 2048
NT = 16
HD = 128
DFF = 5632
NL = 2
INW = 10240
EPS = 1e-6
ENGS = ("pe", "act", "dve", "pool", "sp")
KB = 1024


class T:
    __slots__ = ("name", "w", "r", "dsem")

    def __init__(self, name):
        self.name = name
        self.w = []
        self.r = []
        self.dsem = None


class Prog:
    def __init__(self, nc):
        self.nc = nc
        self.q = {e: [] for e in ENGS}
        self.cnt = {}
        self.seen = {e: {} for e in ENGS}
        self.sem_keys = []
        for e in ("pe", "act", "dve", "pool"):
            self._newsem("E_" + e)
        self.dfree = []
        self.dused = []
        self.flip = 0

    def _newsem(self, key):
        self.cnt[key] = 0
        self.sem_keys.append(key)
        return key

    def dsem(self, t):
        if t.dsem is None:
            if self.dfree:
                t.dsem = self.dfree.pop()
            else:
                t.dsem = self._newsem("D%d" % len(self.sem_keys))
            self.dused.append(t.dsem)
        return t.dsem

    def _waits(self, eng, R, W):
        ws = {}
        seen = self.seen[eng]

        def add(tok):
            k, v = tok
            if eng == "pe" and k == "E_pe":
                return
            if seen.get(k, 0) >= v:
                return
            if ws.get(k, 0) < v:
                ws[k] = v
        for t in R:
            for tok in t.w:
                add(tok)
        for t in W:
            for tok in t.w:
                add(tok)
            for tok in t.r:
                add(tok)
        for k, v in ws.items():
            seen[k] = v
        return list(ws.items())

    def _post(self, tok, R, W, accum):
        for t in R:
            t.r.append(tok)
        for t in W:
            if accum:
                t.w.append(tok)
            else:
                t.w = [tok]
                t.r = []

    def emit(self, eng, fn, R=(), W=(), accum=False):
        waits = self._waits(eng, R, W)
        key = "E_" + eng
        self.cnt[key] += 1
        tok = (key, self.cnt[key])
        self.q[eng].append((waits, fn, (key, 1)))
        self._post(tok, R, W, accum)
        return tok

    def dma(self, queue, out, in_, R=(), W=(), semt=None, accum=False):
        waits = self._waits(queue, R, W)
        key = self.dsem(semt)
        self.cnt[key] += 16
        tok = (key, self.cnt[key])
        self.q[queue].append((waits, lambda e: e.dma_start(out=out, in_=in_), (key, 16)))
        self._post(tok, R, W, accum)
        return tok

    def barrier(self):
        for eng in ENGS:
            ws = []
            for k in self.sem_keys:
                v = self.cnt[k]
                if v > self.seen[eng].get(k, 0):
                    ws.append((k, v))
                    self.seen[eng][k] = v
            if ws:
                self.q[eng].append((ws, None, None))
        self.dfree.extend(self.dused)
        self.dused = []

    def mm(self, out, lhsT, rhs, start, stop, R, W):
        return self.emit("pe", lambda e: e.matmul(out, lhsT=lhsT, rhs=rhs, start=start, stop=stop), R, W)

    def tr(self, out, in_, ident, R, W):
        return self.emit("pe", lambda e: e.transpose(out, in_, ident), R, W)

    def act(self, out, in_, func, R, W, scale=1.0, bias=None, accum_out=None, accum=False):
        def f(e):
            kw = {}
            if bias is not None:
                kw["bias"] = bias
            if accum_out is not None:
                kw["accum_out"] = accum_out
            return e.activation(out=out, in_=in_, func=func, scale=scale, **kw)
        return self.emit("act", f, R, W, accum)

    def tcopy(self, eng, out, in_, R, W, accum=False):
        if eng == "act":
            return self.emit("act", lambda e: e.copy(out, in_), R, W, accum)
        return self.emit(eng, lambda e: e.tensor_copy(out, in_), R, W, accum)

    def tt(self, eng, out, in0, in1, op, R, W, accum=False):
        return self.emit(eng, lambda e: e.tensor_tensor(out, in0, in1, op), R, W, accum)

    def ts(self, eng, out, in0, s1, s2, op0, op1, R, W, accum=False):
        if s2 is None:
            return self.emit(eng, lambda e: e.tensor_scalar(out, in0, s1, None, op0), R, W, accum)
        return self.emit(eng, lambda e: e.tensor_scalar(out, in0, s1, s2, op0, op1), R, W, accum)

    def stt(self, out, in0, scalar, in1, op0, op1, R, W, accum=False):
        return self.emit("dve", lambda e: e.scalar_tensor_tensor(out, in0, scalar, in1, op0, op1), R, W, accum)

    def evac(self, out, in_, R, W, accum=False):
        self.flip ^= 1
        return self.tcopy("act" if self.flip else "dve", out, in_, R, W, accum)

    def build(self):
        nc = self.nc
        with contextlib.ExitStack() as es:
            sems = {}
            for k in self.sem_keys:
                sems[k] = es.enter_context(nc.semaphore(k))
            block = es.enter_context(nc.Block())
            prog = self

            def run(engname):
                def body(eng):
                    for waits, fn, inc in prog.q[engname]:
                        for k, v in waits:
                            eng.wait_ge(sems[k], v)
                        if fn is not None:
                            ins = fn(eng)
                            if inc is not None:
                                ins.then_inc(sems[inc[0]], inc[1])
                return body
            block.tensor(run("pe"))
            block.scalar(run("act"))
            block.vector(run("dve"))
            block.gpsimd(run("pool"))
            block.sync(run("sp"))


class Ring:
    def __init__(self, items):
        self.items = items
        self.i = 0

    def next(self):
        it = self.items[self.i % len(self.items)]
        self.i += 1
        return it


def _prod(xs):
    p = 1
    for x in xs:
        p *= x
    return p


def build_program(nlayers=NL, dbg=False, stop_after=None):
    nc = bass.Bass("TRN2", target_bir_lowering=False)
    P = Prog(nc)

    def din(name, shape, dt=F32):
        return nc.dram_tensor(name, list(shape), dt, kind="ExternalInput").ap()

    skind = "ExternalOutput" if dbg else "Internal"

    def dscr(name, shape, dt=BF16):
        return nc.dram_tensor(name, list(shape), dt, kind=skind).ap()

    x_d = din("x", [S, D])
    ct_d = din("ct", [128, 16])
    wada_d = din("w_ada", [NL, D, 6 * D])
    bada_d = din("b_ada", [NL, 6 * D])
    badaT_d = din("b_adaT", [NL, 128, 96])
    n1_d = din("n1T", [NL, 128, 16])
    n2_d = din("n2T", [NL, 128, 16])
    win_d = din("w_in", [NL, D, INW])
    qn_d = din("qn_g", [NL, 128])
    kn_d = din("kn_g", [NL, 128])
    wba_d = din("w_branch_a", [NL, 512, D])
    wbb_d = din("w_branch_b", [NL, 512, D])
    wout_d = din("w_out", [NL, D, D])
    wgu_d = din("w_gate_up", [NL, D, 2 * DFF])
    wdn_d = din("w_down", [NL, DFF, D])
    cs_d = din("rope_cs", [S, 128])
    sn_d = din("rope_sn", [S, 128])
    y_d = nc.dram_tensor("y", [S, D], F32, kind="ExternalOutput").ap()

    qTA_d = dscr("s_qTA", [12, 128, S])
    kTA_d = dscr("s_kTA", [12, 128, S])
    vA_d = dscr("s_vA", [16, 128, 1536])
    qTB_d = dscr("s_qTB", [4, 128, S])
    kTB_d = dscr("s_kTB", [4, 128, S])
    vB_d = dscr("s_vB", [16, 128, 512])
    gaT_d = dscr("s_gaT", [16, 128, S])
    gbT_d = dscr("s_gbT", [16, 128, S])
    aT_d = dscr("s_aT", [44, 128, S])
    if dbg:
        dbg_uT = dscr("s_uT", [16, 128, S])
        dbg_oT = dscr("s_oT", [8, 128, S])
        dbg_mod = dscr("s_mod", [128, 64 + 4096], F32)

    ARENA = 190 * KB
    arena = nc.alloc_sbuf_tensor("arena", [128, ARENA], U8)

    def carve(off, shape, dt):
        sz = mybir.dt.size(dt)
        n = _prod(shape[1:])
        assert off % 32 == 0 and off + n * sz <= ARENA, (off, shape)
        ap = arena.bitcast(dt)[:, off // sz: off // sz + n]
        if len(shape) == 3:
            ap = ap.rearrange("p (a b) -> p a b", a=shape[1])
        return ap

    R_CONST = 0
    R_MOD = 6 * KB
    R_BIG = 23 * KB
    R_WS = 87 * KB
    R_O = 135 * KB
    R_LOC = 167 * KB

    ident = carve(R_CONST, [128, 128], BF16)
    ones = carve(R_CONST + 256, [128, 128], BF16)
    band = carve(R_CONST + 512, [128, 256], BF16)
    ones_row = carve(R_CONST + 1024, [128, 2048], BF16)
    cact = carve(R_CONST + 5 * KB, [128, 16], BF16)
    epsT = carve(R_CONST + 5 * KB + 64, [128, 1], F32)
    modT = carve(R_MOD, [128, 64], F32)
    G1b = carve(R_MOD + 512, [128, 2048], F32)
    G2b = carve(R_MOD + 512 + 8 * KB, [128, 2048], F32)
    BIG = carve(R_BIG, [128, 16, 2048], BF16)
    WS = [carve(R_WS + i * 16 * KB, [128, 16, 512], BF16) for i in range(3)]

    banks_h = [nc.alloc_psum_tensor("bank%d" % i, [128, 512], F32) for i in range(8)]
    banks = [b[:, :] for b in banks_h]
    banks_bf = [b.bitcast(BF16)[:, :] for b in banks_h]
    Tb = [T("bank%d" % i) for i in range(8)]

    SCALE = float(HD ** -0.5)

    Tc = T("const")
    tmpf = carve(R_LOC, [128, 256], F32)
    P.emit("pool", lambda e: e.memset(tmpf[:, 0:128], 0.0), W=[Tc])
    P.emit("pool", lambda e: e.affine_select(out=tmpf[:, 0:128], in_=tmpf[:, 0:128], compare_op=ALU.not_equal,
                                             fill=1.0, base=0, pattern=[[-1, 128]], channel_multiplier=1),
           R=[Tc], W=[Tc])
    P.tcopy("dve", ident, tmpf[:, 0:128], [Tc], [Tc])
    P.emit("pool", lambda e: e.memset(ones, 1.0), R=[Tc], W=[Tc])
    P.emit("pool", lambda e: e.memset(ones_row, 1.0), R=[Tc], W=[Tc])
    P.emit("pool", lambda e: e.memset(epsT, EPS), R=[Tc], W=[Tc])
    P.emit("pool", lambda e: e.memset(tmpf, 1.0), R=[Tc], W=[Tc])
    P.emit("pool", lambda e: e.affine_select(out=tmpf, in_=tmpf, compare_op=ALU.is_ge, fill=0.0, base=0,
                                             pattern=[[1, 256]], channel_multiplier=-1), R=[Tc], W=[Tc])
    P.emit("pool", lambda e: e.affine_select(out=tmpf, in_=tmpf, compare_op=ALU.is_ge, fill=0.0, base=128,
                                             pattern=[[-1, 256]], channel_multiplier=1), R=[Tc], W=[Tc])
    P.tcopy("dve", band, tmpf, [Tc], [Tc])
    ctf = carve(R_LOC + 2 * KB, [128, 16], F32)
    P.dma("sp", ctf, ct_d, W=[Tc], semt=Tc)
    P.act(cact, ctf, AF.Silu, [Tc], [Tc])
    P.barrier()

    def stop(name):
        return stop_after == name

    def finish():
        P.barrier()
        P.build()
        return nc

    def phase_mod(l):
        Tm = T("mod")
        Crep = carve(R_BIG, [128, 16, 128], BF16)
        for kc in range(16):
            P.tcopy("dve", Crep[:, kc, :], cact[:, kc:kc + 1].to_broadcast([128, 128]), [], [Tm], accum=True)
        wv = wada_d[l].rearrange("(kc p) n -> p kc n", p=128)
        Tw = [T("w%d" % i) for i in range(3)]
        Tbb = [T("bb%d" % i) for i in range(2)]
        bbs = [carve(R_LOC + i * 2 * KB, [128, 512], F32) for i in range(2)]
        badT = carve(R_LOC + 4 * KB, [128, 96], F32)
        Tbad = T("badT")
        P.dma("sp", badT, badaT_d[l], W=[Tbad], semt=Tbad)
        n1 = carve(R_LOC + 5 * KB, [128, 16], F32)
        n2 = carve(R_LOC + 5 * KB + 64, [128, 16], F32)
        P.dma("sp", n1, n1_d[l], W=[Tbad], semt=Tbad, accum=True)
        P.dma("sp", n2, n2_d[l], W=[Tbad], semt=Tbad, accum=True)

        def load(ch):
            if ch < 24:
                P.dma("pool", WS[ch % 3], wv[:, :, ch * 512:(ch + 1) * 512], W=[Tw[ch % 3]], semt=Tw[ch % 3])
        load(0)
        load(1)
        mbank, Tmb = banks[7], Tb[7]
        nb = 0
        ng = 0
        for ch in range(24):
            load(ch + 2)
            w, tw = WS[ch % 3], Tw[ch % 3]
            v = ch // 4
            if v in (2, 5):
                bk, tbk = banks[nb % 2], Tb[nb % 2]
                nb += 1
                for kc in range(16):
                    P.mm(bk, Crep[:, kc, :], w[:, kc, :], kc == 0, kc == 15, [tw, Tm], [tbk])
                bb, tbb = bbs[ng % 2], Tbb[ng % 2]
                ng += 1
                P.dma("sp", bb, bada_d[l, ch * 512:(ch + 1) * 512].partition_broadcast(128), W=[tbb], semt=tbb)
                G = G1b if v == 2 else G2b
                c0 = (ch % 4) * 512
                P.tt("dve", G[:, c0:c0 + 512], bk, bb, ALU.add, [tbk, tbb], [Tm], accum=True)
            else:
                vp = {0: 0, 1: 1, 3: 2, 4: 3}[v]
                for cb in range(4):
                    col = vp * 16 + (ch % 4) * 4 + cb
                    for kc in range(16):
                        P.mm(mbank[:, col:col + 1], w[:, kc, cb * 128:(cb + 1) * 128], cact[:, kc:kc + 1],
                             kc == 0, kc == 15, [tw, Tc], [Tmb])
        for vp, v in enumerate((0, 1, 3, 4)):
            P.tt("dve", modT[:, vp * 16:(vp + 1) * 16], mbank[:, vp * 16:(vp + 1) * 16],
                 badT[:, v * 16:(v + 1) * 16], ALU.add, [Tmb, Tbad], [Tm], accum=True)
        P.stt(modT[:, 16:32], modT[:, 16:32], 1.0, n1, ALU.add, ALU.mult, [Tm, Tbad], [Tm])
        P.stt(modT[:, 48:64], modT[:, 48:64], 1.0, n2, ALU.add, ALU.mult, [Tm, Tbad], [Tm])
        if dbg and l == 0:
            P.dma("sp", dbg_mod[:, 0:64], modT, R=[Tm], semt=Tm)
            P.dma("sp", dbg_mod[:, 64:64 + 2048], G1b, R=[Tm], semt=Tm)
            P.dma("sp", dbg_mod[:, 64 + 2048:64 + 4096], G2b, R=[Tm], semt=Tm)
        P.barrier()

    def phase_norm(h_src, Scol, SHcol):
        hb = [carve(R_O + i * 8 * KB, [128, 2048], F32) for i in range(2)]
        Th = [T("h%d" % i) for i in range(2)]
        xh = [carve(R_O + 16 * KB + i * 4 * KB, [128, 2048], BF16) for i in range(4)]
        Tx = [T("xh%d" % i) for i in range(4)]
        junk = carve(R_LOC, [128, 2048], BF16)
        Tj = T("junk")
        st = [carve(R_LOC + 4 * KB + i * 32, [128, 4], F32) for i in range(4)]
        Ts = [T("st%d" % i) for i in range(4)]
        TuT = T("uT")
        trr = Ring(list(range(4)))

        def load(tt):
            if tt < NT:
                P.dma("sp", hb[tt % 2], h_src[tt * 128:(tt + 1) * 128, :], W=[Th[tt % 2]], semt=Th[tt % 2])
        load(0)
        for tq in range(4):
            for t4 in range(4):
                tt = tq * 4 + t4
                load(tt + 1)
                h, th = hb[tt % 2], Th[tt % 2]
                s, tsn = st[t4], Ts[t4]
                P.emit("pool", lambda e, s=s: e.memset(s, 0.0), W=[tsn])
                P.act(junk, h, AF.Square, [th, tsn], [Tj, tsn], accum_out=s[:, 0:1])
                P.act(s[:, 1:2], s[:, 0:1], AF.Sqrt, [tsn], [tsn], scale=1.0 / D, bias=epsT)
                P.emit("dve", lambda e, s=s: e.reciprocal(s[:, 2:3], s[:, 1:2]), R=[tsn], W=[tsn])
                P.ts("dve", xh[t4], h, s[:, 2:3], None, ALU.mult, None, [th, tsn], [Tx[t4]])
            for j in range(16):
                b = trr.next()
                for t4 in range(4):
                    P.tr(banks_bf[b][:, t4 * 128:(t4 + 1) * 128], xh[t4][:, j * 128:(j + 1) * 128], ident,
                         [Tx[t4], Tc], [Tb[b]])
                out = BIG[:, j, tq * 512:(tq + 1) * 512]
                if j % 2 == 0:
                    P.act(out, banks_bf[b][:, 0:512], AF.Identity, [Tb[b]], [TuT], scale=modT[:, Scol + j:Scol + j + 1],
                          bias=modT[:, SHcol + j:SHcol + j + 1], accum=True)
                else:
                    P.ts("dve", out, banks_bf[b][:, 0:512], modT[:, Scol + j:Scol + j + 1],
                         modT[:, SHcol + j:SHcol + j + 1], ALU.mult, ALU.add, [Tb[b]], [TuT], accum=True)
        P.barrier()

    def phase_inproj(l):
        TuT = T("uT")
        Tw = [T("w%d" % i) for i in range(3)]
        wv = win_d[l].rearrange("(kc p) n -> p kc n", p=128)
        tabs = [carve(R_O + i * 8 * KB, [128, 16, 128], F32) for i in range(4)]
        Ttab = T("tabs")
        gq = carve(R_LOC + 20 * KB, [128, 128], F32)
        gk = carve(R_LOC + 20 * KB + 512, [128, 128], F32)
        gqs = carve(R_LOC + 21 * KB, [128, 128], F32)
        gks = carve(R_LOC + 21 * KB + 512, [128, 128], F32)
        Tg = T("gains")
        P.dma("sp", gq, qn_d[l].partition_broadcast(128), W=[Tg], semt=Tg)
        P.dma("sp", gk, kn_d[l].partition_broadcast(128), W=[Tg], semt=Tg, accum=True)
        P.dma("sp", gqs[:, 0:64], qn_d[l, 64:128].partition_broadcast(128), W=[Tg], semt=Tg, accum=True)
        P.dma("sp", gqs[:, 64:128], qn_d[l, 0:64].partition_broadcast(128), W=[Tg], semt=Tg, accum=True)
        P.dma("sp", gks[:, 0:64], kn_d[l, 64:128].partition_broadcast(128), W=[Tg], semt=Tg, accum=True)
        P.dma("sp", gks[:, 64:128], kn_d[l, 0:64].partition_broadcast(128), W=[Tg], semt=Tg, accum=True)
        csv = cs_d.rearrange("(t p) c -> p t c", p=128)
        snv = sn_d.rearrange("(t p) c -> p t c", p=128)
        for i, (src, g) in enumerate(((csv, gq), (snv, gqs), (csv, gk), (snv, gks))):
            P.dma("sp", tabs[i], src, W=[Ttab], semt=Ttab, accum=True)
        for i, g in enumerate((gq, gqs, gk, gks)):
            P.tt("pool", tabs[i], tabs[i], g.unsqueeze(1).to_broadcast([128, 16, 128]), ALU.mult, [Ttab, Tg], [Ttab])

        X = [carve(R_LOC + i * 2 * KB, [128, 4, 128], F32) for i in range(2)]
        t1 = [carve(R_LOC + 4 * KB + i * 2 * KB, [128, 4, 128], F32) for i in range(2)]
        t2 = [carve(R_LOC + 8 * KB + i * 2 * KB, [128, 4, 128], F32) for i in range(2)]
        ob = [carve(R_LOC + 12 * KB + i * KB, [128, 4, 128], BF16) for i in range(2)]
        sT = [carve(R_LOC + 14 * KB + i * KB, [128, 4, 128], BF16) for i in range(2)]
        stg = [carve(R_LOC + 16 * KB + i * KB, [128, 512], BF16) for i in range(3)]
        rs = [carve(R_LOC + 19 * KB + i * 64, [128, 12], F32) for i in range(2)]
        TX = [T("X%d" % i) for i in range(2)]
        Tt1 = [T("t1%d" % i) for i in range(2)]
        Tt2 = [T("t2%d" % i) for i in range(2)]
        Tob = [T("ob%d" % i) for i in range(2)]
        TsT = [T("sT%d" % i) for i in range(2)]
        Tstg = [T("stg%d" % i) for i in range(3)]
        Trs = [T("rs%d" % i) for i in range(2)]
        mmr = Ring(list(range(6)))
        trr = Ring([6, 7])
        rp = [0]
        sg = [0]

        def load(cc):
            if cc < 20:
                P.dma("pool", WS[cc % 3], wv[:, :, cc * 512:(cc + 1) * 512], W=[Tw[cc % 3]], semt=Tw[cc % 3])

        def rope(b, tt, isk, hc, dst):
            i = rp[0] % 2
            rp[0] += 1
            Xi, t1i, t2i, obi, sTi, rsi = X[i], t1[i], t2[i], ob[i], sT[i], rs[i]
            T1 = tabs[2 * isk][:, tt, :]
            T2 = tabs[2 * isk + 1][:, tt, :]
            bk = banks[b].rearrange("p (a b) -> p a b", a=4)
            P.tcopy("act", Xi, bk, [Tb[b]], [TX[i]])
            P.act(t1i, Xi, AF.Square, [TX[i]], [Tt1[i]])
            P.emit("dve", lambda e: e.tensor_reduce(out=rsi[:, 0:4], in_=t1i, axis=AX.X, op=ALU.add),
                   R=[Tt1[i]], W=[Trs[i]])
            P.act(rsi[:, 4:8], rsi[:, 0:4], AF.Sqrt, [Trs[i]], [Trs[i]], scale=1.0 / HD, bias=epsT)
            P.emit("dve", lambda e: e.reciprocal(rsi[:, 8:12], rsi[:, 4:8]), R=[Trs[i]], W=[Trs[i]])
            P.tt("dve", t1i, Xi, T1.unsqueeze(1).to_broadcast([128, 4, 128]), ALU.mult, [TX[i], Ttab, Trs[i]], [Tt1[i]])
            P.tt("pool", t2i[:, :, 0:64], Xi[:, :, 64:128], T2[:, 0:64].unsqueeze(1).to_broadcast([128, 4, 64]),
                 ALU.mult, [TX[i], Ttab], [Tt2[i]])
            P.tt("pool", t2i[:, :, 64:128], Xi[:, :, 0:64], T2[:, 64:128].unsqueeze(1).to_broadcast([128, 4, 64]),
                 ALU.mult, [TX[i], Ttab], [Tt2[i]], accum=True)
            P.tt("dve", t1i, t1i, t2i, ALU.add, [Tt1[i], Tt2[i]], [Tt1[i]])
            P.tt("dve", obi, t1i, rsi[:, 8:12].unsqueeze(2).to_broadcast([128, 4, 128]), ALU.mult,
                 [Tt1[i], Trs[i]], [Tob[i]])
            tb = trr.next()
            for hh in range(4):
                P.tr(banks_bf[tb][:, hh * 128:(hh + 1) * 128], obi[:, hh, :], ident, [Tob[i], Tc], [Tb[tb]])
            P.evac(sTi, banks_bf[tb][:, 0:512].rearrange("p (a b) -> p a b", a=4), [Tb[tb]], [TsT[i]])
            P.dma("sp", dst[hc * 4:(hc + 1) * 4, :, tt * 128:(tt + 1) * 128].rearrange("h p t -> p h t"), sTi,
                  R=[TsT[i]], semt=TsT[i])

        load(0)
        load(1)
        for cc in range(20):
            load(cc + 2)
            w, tw = WS[cc % 3], Tw[cc % 3]
            if cc < 6:
                isk = cc // 3
                for tt in range(NT):
                    b = mmr.next()
                    for kc in range(16):
                        P.mm(banks[b], BIG[:, kc, tt * 128:(tt + 1) * 128], w[:, kc, :], kc == 0, kc == 15,
                             [tw, TuT], [Tb[b]])
                    rope(b, tt, isk, cc % 3, kTA_d if isk else qTA_d)
            elif cc < 9:
                g = cc - 6
                d = (1, 4, 16)[g]
                nbk = 16 // d
                for idx in range(16):
                    r, n = idx // nbk, idx % nbk
                    st0 = r + d * 128 * n
                    b = mmr.next()
                    for kc in range(16):
                        P.mm(banks[b], BIG[:, kc, st0:st0 + d * 127 + 1:d], w[:, kc, :], kc == 0, kc == 15,
                             [tw, TuT], [Tb[b]])
                    i = sg[0] % 3
                    sg[0] += 1
                    P.evac(stg[i], banks[b], [Tb[b]], [Tstg[i]])
                    P.dma("sp", vA_d[idx, :, g * 512:(g + 1) * 512], stg[i], R=[Tstg[i]], semt=Tstg[i])
            elif cc == 11:
                for tt in range(NT):
                    b = mmr.next()
                    for kc in range(16):
                        P.mm(banks[b], BIG[:, kc, tt * 128:(tt + 1) * 128], w[:, kc, :], kc == 0, kc == 15,
                             [tw, TuT], [Tb[b]])
                    i = sg[0] % 3
                    sg[0] += 1
                    P.evac(stg[i], banks[b], [Tb[b]], [Tstg[i]])
                    P.dma("sp", vB_d[tt], stg[i], R=[Tstg[i]], semt=Tstg[i])
            else:
                for cb in range(4):
                    if cc == 9:
                        dst, sig = qTB_d[cb], False
                    elif cc == 10:
                        dst, sig = kTB_d[cb], False
                    elif cc < 16:
                        dst, sig = gaT_d[(cc - 12) * 4 + cb], True
                    else:
                        dst, sig = gbT_d[(cc - 16) * 4 + cb], True
                    for tq in range(4):
                        b = mmr.next()
                        for kc in range(16):
                            P.mm(banks[b], w[:, kc, cb * 128:(cb + 1) * 128], BIG[:, kc, tq * 512:(tq + 1) * 512],
                                 kc == 0, kc == 15, [tw, TuT], [Tb[b]])
                        i = sg[0] % 3
                        sg[0] += 1
                        if sig:
                            P.act(stg[i], banks[b], AF.Sigmoid, [Tb[b]], [Tstg[i]])
                        else:
                            P.evac(stg[i], banks[b], [Tb[b]], [Tstg[i]])
                        P.dma("sp", dst[:, tq * 512:(tq + 1) * 512], stg[i], R=[Tstg[i]], semt=Tstg[i])
        P.barrier()

    OAT = carve(R_O, [128, 4, 2048], BF16)
    OBT = carve(R_O + 16 * KB, [128, 4, 2048], BF16)

    def phase_mixer_a():
        numacc = carve(R_BIG, [128, 4, 2048], F32)
        denacc = carve(R_BIG + 32 * KB, [128, 4, 2048], F32)
        Tacc = [T("acc%d" % j) for j in range(4)]
        Tdacc = [T("dacc%d" % j) for j in range(4)]
        for j in range(4):
            P.emit("pool", lambda e, j=j: e.memset(numacc[:, j, :], 0.0), W=[Tacc[j]])
            P.emit("pool", lambda e, j=j: e.memset(denacc[:, j, :], 0.0), W=[Tdacc[j]])
        Vb = [carve(R_WS + i * 16 * KB, [128, 16, 512], BF16) for i in range(2)]
        TV = [T("V%d" % i) for i in range(2)]
        QK = [carve(R_WS + 32 * KB + i * 4 * KB, [128, 2048], BF16) for i in range(4)]
        TQK = [T("QK%d" % i) for i in range(4)]
        pT = [carve(R_LOC + i * 512, [128, 256], BF16) for i in range(4)]
        TpT = [T("pT%d" % i) for i in range(4)]
        rec = carve(R_LOC + 4 * KB, [128, 2048], F32)
        Trec = T("rec")
        sring = Ring([0, 1, 2, 3])
        slots = [(4 + s // 4, 6 + s // 4, (s % 4) * 128) for s in range(8)]
        Tns = [T("ns%d" % s) for s in range(8)]
        Tds = [T("ds%d" % s) for s in range(8)]
        sc = [0]
        pc = [0]
        loads = []
        for g in range(3):
            for j in range(4):
                loads.append((g, j))

        def load(i):
            if i < len(loads):
                g, j = loads[i]
                head = g * 4 + j
                if j == 0:
                    P.dma("sp", Vb[g % 2], vA_d[:, :, g * 512:(g + 1) * 512].rearrange("t p c -> p t c"),
                          W=[TV[g % 2]], semt=TV[g % 2])
                q = (i % 2) * 2
                P.dma("sp", QK[q], qTA_d[head], W=[TQK[q]], semt=TQK[q])
                P.dma("sp", QK[q + 1], kTA_d[head], W=[TQK[q + 1]], semt=TQK[q + 1])
        load(0)
        for i, (g, j) in enumerate(loads):
            load(i + 1)
            d = (1, 4, 16)[g]
            nbk = 16 // d
            q = (i % 2) * 2
            QT, KT, tq_, tk_ = QK[q], QK[q + 1], TQK[q], TQK[q + 1]
            V, tv = Vb[g % 2], TV[g % 2]
            for r in range(d):
                cur = None
                for kt in range(nbk):
                    nq = 256 if kt < nbk - 1 else 128
                    k0 = r + d * 128 * kt
                    sb = sring.next()
                    P.mm(banks[sb][:, 0:nq], KT[:, k0:k0 + d * 127 + 1:d], QT[:, k0:k0 + d * (nq - 1) + 1:d],
                         True, True, [tq_, tk_], [Tb[sb]])
                    pi = pc[0] % 4
                    pc[0] += 1
                    P.act(pT[pi][:, 0:nq], banks[sb][:, 0:nq], AF.Exp, [Tb[sb]], [TpT[pi]], scale=SCALE)
                    P.tt("dve", pT[pi][:, 0:nq], pT[pi][:, 0:nq], band[:, 0:nq], ALU.mult, [TpT[pi], Tc], [TpT[pi]])
                    Vl = V[:, r * nbk + kt, j * 128:(j + 1) * 128]
                    if cur is None:
                        cur = sc[0] % 8
                        sc[0] += 1
                        first = True
                    else:
                        first = False
                    nbank, dbank, c0 = slots[cur]
                    P.mm(banks[nbank][:, c0:c0 + 128], Vl, pT[pi][:, 0:128], first, True, [tv, TpT[pi]], [Tns[cur]])
                    P.mm(banks[dbank][:, c0:c0 + 128], ones, pT[pi][:, 0:128], first, True, [Tc, TpT[pi]], [Tds[cur]])
                    tok = slice(k0, k0 + d * 127 + 1, d)
                    P.tt("dve", numacc[:, j, tok], numacc[:, j, tok], banks[nbank][:, c0:c0 + 128], ALU.add,
                         [Tns[cur], Tacc[j]], [Tacc[j]])
                    P.tt("dve", denacc[:, j, tok], denacc[:, j, tok], banks[dbank][:, c0:c0 + 128], ALU.add,
                         [Tds[cur], Tdacc[j]], [Tdacc[j]])
                    if nq == 256:
                        nxt = sc[0] % 8
                        sc[0] += 1
                        nbank2, dbank2, c2 = slots[nxt]
                        P.mm(banks[nbank2][:, c2:c2 + 128], Vl, pT[pi][:, 128:256], True, False,
                             [tv, TpT[pi]], [Tns[nxt]])
                        P.mm(banks[dbank2][:, c2:c2 + 128], ones, pT[pi][:, 128:256], True, False,
                             [Tc, TpT[pi]], [Tds[nxt]])
                        cur = nxt
                    else:
                        cur = None
        for j in range(4):
            P.emit("dve", lambda e, j=j: e.reciprocal(rec, denacc[:, j, :]), R=[Tdacc[j]], W=[Trec])
            P.tt("dve", OAT[:, j, :], numacc[:, j, :], rec, ALU.mult, [Tacc[j], Trec], [Tdacc[j]])
        if dbg:
            for j in range(4):
                P.dma("sp", dbg_oT[j], OAT[:, j, :], R=[Tdacc[j]], semt=Tdacc[j])
        P.barrier()

    def phase_mixer_b():
        fb = [[carve(R_BIG + (s * 4 + i) * 8 * KB, [128, 2048], F32) for i in range(4)] for s in range(2)]
        Tfb = [[T("fb%d_%d" % (s, i)) for i in range(4)] for s in range(2)]
        QKV = [[carve(R_WS + (s * 3 + i) * 4 * KB, [128, 2048], BF16) for i in range(3)] for s in range(2)]
        TQKV = [[T("qkv%d_%d" % (s, i)) for i in range(3)] for s in range(2)]
        Ab = [carve(R_WS + 24 * KB + i * 4 * KB, [128, 2048], BF16) for i in range(2)]
        TA = [T("A%d" % i) for i in range(2)]
        ATb = [carve(R_WS + 32 * KB + i * 4 * KB, [128, 16, 128], BF16) for i in range(2)]
        TAT = [T("AT%d" % i) for i in range(2)]
        TOB = T("OBT")
        zr = Ring([0, 1, 2, 3])
        trr = Ring([4, 5])
        orr = Ring([6, 7])

        def load(hh):
            if hh < 4:
                s = hh % 2
                P.dma("sp", QKV[s][0], qTB_d[hh], W=[TQKV[s][0]], semt=TQKV[s][0])
                P.dma("sp", QKV[s][1], kTB_d[hh], W=[TQKV[s][1]], semt=TQKV[s][1])
                P.dma("sp", QKV[s][2].rearrange("p (t c) -> p t c", t=16),
                      vB_d[:, :, hh * 128:(hh + 1) * 128].rearrange("t p c -> p t c"),
                      W=[TQKV[s][2]], semt=TQKV[s][2])
        load(0)
        it = 0
        for hh in range(4):
            load(hh + 1)
            s = hh % 2
            QT, KT = QKV[s][0], QKV[s][1]
            V = QKV[s][2].rearrange("p (t c) -> p t c", t=16)
            tq_, tk_, tv_ = TQKV[s]
            for qt in range(NT):
                nk = 128 * (qt + 1)
                bs = it % 2
                it += 1
                E, SPt, LB, G = fb[bs]
                tE, tSP, tLB, tG = Tfb[bs]
                A, tA = Ab[bs], TA[bs]
                AT, tAT = ATb[bs], TAT[bs]
                nch = (nk + 511) // 512
                for c in range(nch):
                    w = min(512, nk - c * 512)
                    cs_ = slice(c * 512, c * 512 + w)
                    zb = zr.next()
                    P.mm(banks[zb][:, 0:w], QT[:, qt * 128:(qt + 1) * 128], KT[:, cs_], True, True,
                         [tq_, tk_], [Tb[zb]])
                    P.act(E[:, cs_], banks[zb][:, 0:w], AF.Exp, [Tb[zb]], [tE], scale=SCALE, accum=(c > 0))
                    P.act(SPt[:, cs_], E[:, cs_], AF.Ln, [tE], [tSP], bias=1.0, accum=(c > 0))
                    P.stt(LB[:, cs_], banks[zb][:, 0:w], SCALE, SPt[:, cs_], ALU.mult, ALU.subtract,
                          [Tb[zb], tSP], [tLB], accum=(c > 0))
                dg = slice(nk - 128, nk)
                P.emit("pool", lambda e, SPt=SPt, dg=dg: e.affine_select(
                    out=SPt[:, dg], in_=SPt[:, dg], compare_op=ALU.is_ge, fill=0.0, base=-1,
                    pattern=[[-1, 128]], channel_multiplier=1), R=[tSP, tLB], W=[tSP])
                P.emit("dve", lambda e, G=G, SPt=SPt, nk=nk: e.tensor_tensor_scan(
                    out=G[:, 0:nk], data0=ones_row[:, 0:nk], data1=SPt[:, 0:nk], initial=0.0,
                    op0=ALU.mult, op1=ALU.add), R=[tSP, Tc], W=[tG])
                P.stt(LB[:, 0:nk], G[:, 0:nk], G[:, nk - 1:nk], LB[:, 0:nk], ALU.subtract, ALU.add,
                      [tG, tLB], [tLB])
                P.act(A[:, 0:nk], LB[:, 0:nk], AF.Exp, [tLB], [tA])
                P.emit("pool", lambda e, A=A, dg=dg: e.affine_select(
                    out=A[:, dg], in_=A[:, dg], compare_op=ALU.is_ge, fill=0.0, base=-1,
                    pattern=[[-1, 128]], channel_multiplier=1), R=[tA], W=[tA])
                nblk = qt + 1
                for g0 in range(0, nblk, 4):
                    gn = min(4, nblk - g0)
                    tb = trr.next()
                    for i in range(gn):
                        kb = g0 + i
                        P.tr(banks_bf[tb][:, i * 128:(i + 1) * 128], A[:, kb * 128:(kb + 1) * 128], ident,
                             [tA, Tc], [Tb[tb]])
                    P.evac(AT[:, g0:g0 + gn, :], banks_bf[tb][:, 0:gn * 128].rearrange("p (a b) -> p a b", a=gn),
                           [Tb[tb]], [tAT], accum=(g0 > 0))
                ob_ = orr.next()
                for kb in range(nblk):
                    P.mm(banks[ob_][:, 0:128], V[:, kb, :], AT[:, kb, :], kb == 0, kb == nblk - 1,
                         [tv_, tAT], [Tb[ob_]])
                P.evac(OBT[:, hh, qt * 128:(qt + 1) * 128], banks[ob_][:, 0:128], [Tb[ob_]], [TOB], accum=True)
        if dbg:
            for j in range(4):
                P.dma("sp", dbg_oT[4 + j], OBT[:, j, :], R=[TOB], semt=TOB)
        P.barrier()

    def phase_merge(l):
        MT = BIG
        TMT = T("MT")
        wba = carve(R_WS, [128, 4, 2048], BF16)
        wbb = carve(R_WS + 16 * KB, [128, 4, 2048], BF16)
        Twb = T("wb")
        To = T("o")
        P.dma("pool", wba, wba_d[l].rearrange("(k p) n -> p k n", p=128), W=[Twb], semt=Twb)
        P.dma("pool", wbb, wbb_d[l].rearrange("(k p) n -> p k n", p=128), W=[Twb], semt=Twb, accum=True)
        sg = [carve(R_WS + 32 * KB + i * 4 * KB, [128, 2048], BF16) for i in range(4)]
        Tsg = [T("sg%d" % i) for i in range(4)]
        m1 = [carve(R_LOC + i * 2 * KB, [128, 512], F32) for i in range(2)]
        m2 = [carve(R_LOC + 4 * KB + i * 2 * KB, [128, 512], F32) for i in range(2)]
        Tm1 = [T("m1%d" % i) for i in range(2)]
        Tm2 = [T("m2%d" % i) for i in range(2)]
        br = Ring([0, 1, 2, 3, 4, 5, 6, 7])

        def load(fc):
            if fc < 16:
                s = (fc % 2) * 2
                P.dma("sp", sg[s], gaT_d[fc], W=[Tsg[s]], semt=Tsg[s])
                P.dma("sp", sg[s + 1], gbT_d[fc], W=[Tsg[s + 1]], semt=Tsg[s + 1])
        load(0)
        it = 0
        for fc in range(16):
            load(fc + 1)
            s = (fc % 2) * 2
            for tq in range(4):
                ts_ = slice(tq * 512, (tq + 1) * 512)
                ya = br.next()
                for j in range(4):
                    P.mm(banks[ya], wba[:, j, fc * 128:(fc + 1) * 128], OAT[:, j, ts_], j == 0, j == 3, [Twb, To], [Tb[ya]])
                yb = br.next()
                for j in range(4):
                    P.mm(banks[yb], wbb[:, j, fc * 128:(fc + 1) * 128], OBT[:, j, ts_], j == 0, j == 3, [Twb, To], [Tb[yb]])
                i = it % 2
                it += 1
                P.tt("dve", m1[i], banks[ya], sg[s][:, ts_], ALU.mult, [Tb[ya], Tsg[s]], [Tm1[i]])
                P.tt("dve", m2[i], banks[yb], sg[s + 1][:, ts_], ALU.mult, [Tb[yb], Tsg[s + 1]], [Tm2[i]])
                P.tt("pool", MT[:, fc, ts_], m1[i], m2[i], ALU.add, [Tm1[i], Tm2[i]], [TMT], accum=True)
        P.barrier()

    def resid_update(bank, tbank, Gb, h_src, tt, c, hp, thp, tmp, ttmp):
        rows = slice(tt * 128, (tt + 1) * 128)
        cols = slice(c * 512, (c + 1) * 512)
        P.dma("sp", hp, h_src[rows, cols], W=[thp], semt=thp)
        P.tt("dve", tmp, bank, Gb[:, cols], ALU.mult, [tbank], [ttmp])
        P.tt("pool", hp, hp, tmp, ALU.add, [thp, ttmp], [thp])
        P.dma("sp", y_d[rows, cols], hp, R=[thp], semt=thp)

    def phase_outproj(l, h_src):
        MT = BIG
        TMT = T("MT")
        Tw = [T("w%d" % i) for i in range(3)]
        wv = wout_d[l].rearrange("(kc p) n -> p kc n", p=128)
        hp = [carve(R_LOC + i * 2 * KB, [128, 512], F32) for i in range(4)]
        Thp = [T("hp%d" % i) for i in range(4)]
        tmp = [carve(R_LOC + 8 * KB + i * 2 * KB, [128, 512], F32) for i in range(2)]
        Ttmp = [T("tmp%d" % i) for i in range(2)]
        br = Ring([0, 1, 2, 3, 4, 5, 6, 7])

        def load(c):
            if c < 4:
                P.dma("pool", WS[c % 3], wv[:, :, c * 512:(c + 1) * 512], W=[Tw[c % 3]], semt=Tw[c % 3])
        load(0)
        load(1)
        it = 0
        for c in range(4):
            load(c + 2)
            w, tw = WS[c % 3], Tw[c % 3]
            for tt in range(NT):
                b = br.next()
                for fc in range(16):
                    P.mm(banks[b], MT[:, fc, tt * 128:(tt + 1) * 128], w[:, fc, :], fc == 0, fc == 15, [tw, TMT], [Tb[b]])
                resid_update(banks[b], Tb[b], G1b, h_src, tt, c, hp[it % 4], Thp[it % 4], tmp[it % 2], Ttmp[it % 2])
                it += 1
        P.barrier()

    def phase_gateup(l):
        TuT = T("uT")
        Tw = [T("w%d" % i) for i in range(3)]
        wv = wgu_d[l].rearrange("(kc p) n -> p kc n", p=128)
        sgl = [carve(R_LOC + i * 2 * KB, [128, 512], F32) for i in range(2)]
        Tsgl = [T("sgl%d" % i) for i in range(2)]
        ast = [carve(R_LOC + 4 * KB + i * KB, [128, 512], BF16) for i in range(3)]
        Tast = [T("ast%d" % i) for i in range(3)]
        br = Ring([0, 1, 2, 3, 4, 5, 6, 7])

        def load(st):
            if st < 22:
                i = st % 3
                P.dma("pool", WS[i][:, :, 0:256], wv[:, :, st * 256:(st + 1) * 256], W=[Tw[i]], semt=Tw[i])
                P.dma("pool", WS[i][:, :, 256:512], wv[:, :, DFF + st * 256:DFF + (st + 1) * 256], W=[Tw[i]],
                      semt=Tw[i], accum=True)
        load(0)
        load(1)
        it = 0
        for st in range(22):
            load(st + 2)
            w, tw = WS[st % 3], Tw[st % 3]
            for fb2 in range(2):
                fbi = st * 2 + fb2
                for tq in range(4):
                    ts_ = slice(tq * 512, (tq + 1) * 512)
                    gb_ = br.next()
                    for kc in range(16):
                        P.mm(banks[gb_], w[:, kc, fb2 * 128:(fb2 + 1) * 128], BIG[:, kc, ts_], kc == 0, kc == 15,
                             [tw, TuT], [Tb[gb_]])
                    ub_ = br.next()
                    for kc in range(16):
                        P.mm(banks[ub_], w[:, kc, 256 + fb2 * 128:256 + (fb2 + 1) * 128], BIG[:, kc, ts_], kc == 0,
                             kc == 15, [tw, TuT], [Tb[ub_]])
                    i = it % 2
                    k = it % 3
                    it += 1
                    P.act(sgl[i], banks[gb_], AF.Silu, [Tb[gb_]], [Tsgl[i]])
                    P.tt("dve", ast[k], sgl[i], banks[ub_], ALU.mult, [Tsgl[i], Tb[ub_]], [Tast[k]])
                    P.dma("sp", aT_d[fbi, :, ts_], ast[k], R=[Tast[k]], semt=Tast[k])
        P.barrier()

    def phase_down(l):
        Wd = [carve(R_BIG, [128, 44, 512], BF16), carve(R_WS, [128, 44, 512], BF16)]
        TWd = [T("Wd0"), T("Wd1")]
        wv = wdn_d[l].rearrange("(fb p) n -> p fb n", p=128)
        at = [carve(R_O, [128, 11, 512], BF16), carve(R_O + 11 * KB, [128, 11, 512], BF16),
              carve(R_BIG + 44 * KB, [128, 11, 512], BF16)]
        Tat = [T("at%d" % i) for i in range(3)]
        hp = [carve(R_LOC + i * 2 * KB, [128, 512], F32) for i in range(4)]
        Thp = [T("hp%d" % i) for i in range(4)]
        tmp = [carve(R_LOC + 8 * KB + i * 2 * KB, [128, 512], F32) for i in range(2)]
        Ttmp = [T("tmp%d" % i) for i in range(2)]
        atl = [(c, tq, kg) for c in range(4) for tq in range(4) for kg in range(4)]

        def loadw(c):
            if c < 4:
                for kg in range(4):
                    P.dma("pool", Wd[c % 2][:, kg * 11:(kg + 1) * 11, :], wv[:, kg * 11:(kg + 1) * 11, c * 512:(c + 1) * 512],
                          W=[TWd[c % 2]], semt=TWd[c % 2], accum=(kg > 0))

        def loada(i):
            if i < len(atl):
                c, tq, kg = atl[i]
                P.dma("sp", at[i % 3], aT_d[kg * 11:(kg + 1) * 11, :, tq * 512:(tq + 1) * 512].rearrange("f p t -> p f t"),
                      W=[Tat[i % 3]], semt=Tat[i % 3])
        loadw(0)
        loada(0)
        loada(1)
        ai = 0
        it = 0
        grp = 0
        for c in range(4):
            loadw(c + 1)
            W_, tW = Wd[c % 2], TWd[c % 2]
            for tq in range(4):
                bset = [0, 1, 2, 3] if grp % 2 == 0 else [4, 5, 6, 7]
                grp += 1
                for kg in range(4):
                    loada(ai + 2)
                    a_, ta = at[ai % 3], Tat[ai % 3]
                    ai += 1
                    for t4 in range(4):
                        b = bset[t4]
                        for f in range(11):
                            P.mm(banks[b], a_[:, f, t4 * 128:(t4 + 1) * 128], W_[:, kg * 11 + f, :],
                                 kg == 0 and f == 0, kg == 3 and f == 10, [ta, tW], [Tb[b]])
                for t4 in range(4):
                    b = bset[t4]
                    resid_update(banks[b], Tb[b], G2b, y_d, tq * 4 + t4, c, hp[it % 4], Thp[it % 4], tmp[it % 2], Ttmp[it % 2])
                    it += 1
        P.barrier()

    for l in range(nlayers):
        phase_mod(l)
        if stop("mod"):
            return finish()
        phase_norm(x_d if l == 0 else y_d, 16, 0)
        if dbg and l == 0:
            Td = T("dbg")
            for j in range(16):
                P.dma("sp", dbg_uT[j], BIG[:, j, :], semt=Td)
            P.barrier()
        if stop("norm"):
            return finish()
        phase_inproj(l)
        if stop("inproj"):
            return finish()
        phase_mixer_a()
        if stop("mixa"):
            return finish()
        phase_mixer_b()
        if stop("mixb"):
            return finish()
        phase_merge(l)
        phase_outproj(l, x_d if l == 0 else y_d)
        if stop("outproj"):
            return finish()
        phase_norm(y_d, 48, 32)
        phase_gateup(l)
        phase_down(l)
    return finish()


def host_inputs(inputs):
    f = lambda a: np.ascontiguousarray(np.asarray(a, dtype=np.float32))
    x = f(inputs["x"])
    c = f(inputs["c"])
    B = x.shape[0]
    inv = np.power(np.float32(10000.0), -np.arange(0, HD, 2, dtype=np.float32) / np.float32(HD)).astype(np.float32)
    ang = (np.arange(S, dtype=np.float32)[:, None] * inv[None, :]).astype(np.float32)
    cos = np.cos(ang.astype(np.float64)).astype(np.float32)
    sin = np.sin(ang.astype(np.float64)).astype(np.float32)
    shared = {
        "w_ada": f(inputs["w_ada"]),
        "b_ada": f(inputs["b_ada"]),
        "b_adaT": np.ascontiguousarray(f(inputs["b_ada"]).reshape(NL, 96, 128).transpose(0, 2, 1)),
        "n1T": np.ascontiguousarray(f(inputs["norm1_g"]).reshape(NL, 16, 128).transpose(0, 2, 1)),
        "n2T": np.ascontiguousarray(f(inputs["norm2_g"]).reshape(NL, 16, 128).transpose(0, 2, 1)),
        "w_in": f(inputs["w_in"]),
        "qn_g": f(inputs["qn_g"]),
        "kn_g": f(inputs["kn_g"]),
        "w_branch_a": f(inputs["w_branch_a"]),
        "w_branch_b": f(inputs["w_branch_b"]),
        "w_out": f(inputs["w_out"]),
        "w_gate_up": f(inputs["w_gate_up"]),
        "w_down": f(inputs["w_down"]),
        "rope_cs": np.ascontiguousarray(np.concatenate([cos, cos], axis=1)),
        "rope_sn": np.ascontiguousarray(np.concatenate([-sin, sin], axis=1)),
    }
    in_maps = []
    for b in range(B):
        m = dict(shared)
        m["x"] = x[b]
        m["ct"] = np.ascontiguousarray(c[b].reshape(16, 128).T)
        in_maps.append(m)
    return in_maps


def kernel(**inputs):
    in_maps = host_inputs(inputs)
    nc = build_program()
    res = run_bass_kernel_spmd(nc, in_maps, core_ids=list(range(len(in_maps))))
    return np.stack([np.asarray(r["y"], dtype=np.float32) for r in res.results], axis=0)
```

```python
import contextlib
import numpy as np
import ml_dtypes
import concourse.bass as bass
import concourse.mybir as mybir
from concourse.bass_utils import run_bass_kernel_spmd

F32 = mybir.dt.float32
BF16 = mybir.dt.bfloat16
U8 = mybir.dt.uint8
AF = mybir.ActivationFunctionType
ALU = mybir.AluOpType
AX = mybir.AxisListType

S = 2048
D = 2048
NT = 16
HD = 128
DFF = 5632
NL = 2
INW = 10240
EPS = 1e-6
ENGS = ("pe", "act", "dve", "pool", "sp")
KB = 1024


class T:
    __slots__ = ("name", "w", "r", "dsem")

    def __init__(self, name):
        self.name = name
        self.w = []
        self.r = []
        self.dsem = None


class Prog:
    def __init__(self, nc):
        self.nc = nc
        self.q = {e: [] for e in ENGS}
        self.cnt = {}
        self.seen = {e: {} for e in ENGS}
        self.sem_keys = []
        for e in ("pe", "act", "dve", "pool"):
            self._newsem("E_" + e)
        self.dfree = []
        self.dused = []
        self.flip = 0

    def _newsem(self, key):
        self.cnt[key] = 0
        self.sem_keys.append(key)
        return key

    def dsem(self, t):
        if t.dsem is None:
            if self.dfree:
                t.dsem = self.dfree.pop()
            else:
                t.dsem = self._newsem("D%d" % len(self.sem_keys))
            self.dused.append(t.dsem)
        return t.dsem

    def _waits(self, eng, R, W):
        ws = {}
        seen = self.seen[eng]

        def add(tok):
            k, v = tok
            if eng == "pe" and k == "E_pe":
                return
            if seen.get(k, 0) >= v:
                return
            if ws.get(k, 0) < v:
                ws[k] = v
        for t in R:
            for tok in t.w:
                add(tok)
        for t in W:
            for tok in t.w:
                add(tok)
            for tok in t.r:
                add(tok)
        for k, v in ws.items():
            seen[k] = v
        return list(ws.items())

    def _post(self, tok, R, W, accum):
        for t in R:
            t.r.append(tok)
        for t in W:
            if accum:
                t.w.append(tok)
            else:
                t.w = [tok]
                t.r = []

    def emit(self, eng, fn, R=(), W=(), accum=False):
        waits = self._waits(eng, R, W)
        key = "E_" + eng
        self.cnt[key] += 1
        tok = (key, self.cnt[key])
        self.q[eng].append((waits, fn, (key, self.cnt[key])))
        self._post(tok, R, W, accum)
        return tok

    def dma(self, queue, out, in_, R=(), W=(), semt=None, accum=False):
        waits = self._waits(queue, R, W)
        key = self.dsem(semt)
        self.cnt[key] += 16
        tok = (key, self.cnt[key])
        self.q[queue].append((waits, lambda e: e.dma_start(out=out, in_=in_), (key, 16)))
        self._post(tok, R, W, accum)
        return tok

    def barrier(self):
        for eng in ENGS:
            ws = []
            for k in self.sem_keys:
                v = self.cnt[k]
                if v > self.seen[eng].get(k, 0):
                    ws.append((k, v))
                    self.seen[eng][k] = v
            if ws:
                self.q[eng].append((ws, None, None))
        self.dfree.extend(self.dused)
        self.dused = []

    def mm(self, out, lhsT, rhs, start, stop, R, W):
        return self.emit("pe", lambda e: e.matmul(out, lhsT=lhsT, rhs=rhs, start=start, stop=stop), R, W)

    def tr(self, out, in_, ident, R, W):
        return self.emit("pe", lambda e: e.transpose(out, in_, ident), R, W)

    def act(self, out, in_, func, R, W, scale=1.0, bias=None, accum_out=None, accum=False):
        def f(e):
            kw = {}
            if bias is not None:
                kw["bias"] = bias
            if accum_out is not None:
                kw["accum_out"] = accum_out
            return e.activation(out=out, in_=in_, func=func, scale=scale, **kw)
        return self.emit("act", f, R, W, accum)

    def tcopy(self, eng, out, in_, R, W, accum=False):
        if eng == "act":
            return self.emit("act", lambda e: e.copy(out, in_), R, W, accum)
        return self.emit(eng, lambda e: e.tensor_copy(out, in_), R, W, accum)

    def tt(self, eng, out, in0, in1, op, R, W, accum=False):
        return self.emit(eng, lambda e: e.tensor_tensor(out, in0, in1, op), R, W, accum)

    def ts(self, eng, out, in0, s1, s2, op0, op1, R, W, accum=False):
        if s2 is None:
            return self.emit(eng, lambda e: e.tensor_scalar(out, in0, s1, None, op0), R, W, accum)
        return self.emit(eng, lambda e: e.tensor_scalar(out, in0, s1, s2, op0, op1), R, W, accum)

    def stt(self, out, in0, scalar, in1, op0, op1, R, W, accum=False):
        return self.emit("dve", lambda e: e.scalar_tensor_tensor(out, in0, scalar, in1, op0, op1), R, W, accum)

    def evac(self, out, in_, R, W, accum=False):
        self.flip ^= 1
        return self.tcopy("act" if self.flip else "dve", out, in_, R, W, accum)

    def build(self):
        nc = self.nc
        flagged = {k: set() for k in self.sem_keys if k.startswith("E_")}
        for e in ENGS:
            for waits, fn, inc in self.q[e]:
                for k, v in waits:
                    if k in flagged:
                        flagged[k].add(v)
        rank = {}
        for k, st in flagged.items():
            rank[k] = {v: i + 1 for i, v in enumerate(sorted(st))}
        with contextlib.ExitStack() as es:
            sems = {}
            for k in self.sem_keys:
                sems[k] = es.enter_context(nc.semaphore(k))
            block = es.enter_context(nc.Block())
            prog = self

            def run(engname):
                def body(eng):
                    for waits, fn, inc in prog.q[engname]:
                        for k, v in waits:
                            if k in rank:
                                v = rank[k][v]
                            eng.wait_ge(sems[k], v)
                        if fn is not None:
                            ins = fn(eng)
                            if inc is not None:
                                k, n = inc
                                if k in rank:
                                    if n in rank[k]:
                                        ins.then_inc(sems[k], 1)
                                else:
                                    ins.then_inc(sems[k], n)
                return body
            block.tensor(run("pe"))
            block.scalar(run("act"))
            block.vector(run("dve"))
            block.gpsimd(run("pool"))
            block.sync(run("sp"))


class Ring:
    def __init__(self, items):
        self.items = items
        self.i = 0

    def next(self):
        it = self.items[self.i % len(self.items)]
        self.i += 1
        return it


def _prod(xs):
    p = 1
    for x in xs:
        p *= x
    return p


def build_program(nlayers=NL, dbg=False, stop_after=None):
    nc = bass.Bass("TRN2", target_bir_lowering=False)
    P = Prog(nc)

    def din(name, shape, dt=F32):
        return nc.dram_tensor(name, list(shape), dt, kind="ExternalInput").ap()

    skind = "ExternalOutput" if dbg else "Internal"

    def dscr(name, shape, dt=BF16):
        return nc.dram_tensor(name, list(shape), dt, kind=skind).ap()

    x_d = din("x", [S, D])
    ct_d = din("ct", [128, 16])
    wada_d = din("w_ada", [NL, D, 6 * D])
    bada_d = din("b_ada", [NL, 6 * D])
    badaT_d = din("b_adaT", [NL, 128, 96])
    n1_d = din("n1T", [NL, 128, 16])
    n2_d = din("n2T", [NL, 128, 16])
    win_d = din("w_in", [NL, D, INW])
    qn_d = din("qn_g", [NL, 128])
    kn_d = din("kn_g", [NL, 128])
    wba_d = din("w_branch_a", [NL, 512, D])
    wbb_d = din("w_branch_b", [NL, 512, D])
    wout_d = din("w_out", [NL, D, D])
    wgu_d = din("w_gate_up", [NL, D, 2 * DFF])
    wdn_d = din("w_down", [NL, DFF, D])
    cs_d = din("rope_cs", [S, 128])
    sn_d = din("rope_sn", [S, 128])
    y_d = nc.dram_tensor("y", [S, D], F32, kind="ExternalOutput").ap()

    qTA_d = dscr("s_qTA", [12, 128, S])
    kTA_d = dscr("s_kTA", [12, 128, S])
    vA_d = dscr("s_vA", [16, 128, 1536])
    qTB_d = dscr("s_qTB", [4, 128, S])
    kTB_d = dscr("s_kTB", [4, 128, S])
    vB_d = dscr("s_vB", [16, 128, 512])
    gaT_d = dscr("s_gaT", [16, 128, S])
    gbT_d = dscr("s_gbT", [16, 128, S])
    aT_d = dscr("s_aT", [44, 128, S])
    if dbg:
        dbg_uT = dscr("s_uT", [16, 128, S])
        dbg_oT = dscr("s_oT", [8, 128, S])
        dbg_mod = dscr("s_mod", [128, 64 + 4096], F32)

    ARENA = 190 * KB
    arena = nc.alloc_sbuf_tensor("arena", [128, ARENA], U8)

    def carve(off, shape, dt):
        sz = mybir.dt.size(dt)
        n = _prod(shape[1:])
        assert off % 32 == 0 and off + n * sz <= ARENA, (off, shape)
        ap = arena.bitcast(dt)[:, off // sz: off // sz + n]
        if len(shape) == 3:
            ap = ap.rearrange("p (a b) -> p a b", a=shape[1])
        return ap

    R_CONST = 0
    R_MOD = 6 * KB
    R_BIG = 23 * KB
    R_WS = 87 * KB
    R_O = 135 * KB
    R_LOC = 167 * KB

    ident = carve(R_CONST, [128, 128], BF16)
    ones = carve(R_CONST + 256, [128, 128], BF16)
    band = carve(R_CONST + 512, [128, 256], BF16)
    ones_row = carve(R_CONST + 1024, [128, 2048], BF16)
    cact = carve(R_CONST + 5 * KB, [128, 16], BF16)
    epsT = carve(R_CONST + 5 * KB + 64, [128, 1], F32)
    modT = carve(R_MOD, [128, 64], F32)
    G1b = carve(R_MOD + 512, [128, 2048], F32)
    G2b = carve(R_MOD + 512 + 8 * KB, [128, 2048], F32)
    BIG = carve(R_BIG, [128, 16, 2048], BF16)
    WS = [carve(R_WS + i * 16 * KB, [128, 16, 512], BF16) for i in range(3)]

    banks_h = [nc.alloc_psum_tensor("bank%d" % i, [128, 512], F32) for i in range(8)]
    banks = [b[:, :] for b in banks_h]
    banks_bf = [b.bitcast(BF16)[:, :] for b in banks_h]
    Tb = [T("bank%d" % i) for i in range(8)]

    SCALE = float(HD ** -0.5)

    Tc = T("const")
    tmpf = carve(R_LOC, [128, 256], F32)
    P.emit("pool", lambda e: e.memset(tmpf[:, 0:128], 0.0), W=[Tc])
    P.emit("pool", lambda e: e.affine_select(out=tmpf[:, 0:128], in_=tmpf[:, 0:128], compare_op=ALU.not_equal,
                                             fill=1.0, base=0, pattern=[[-1, 128]], channel_multiplier=1),
           R=[Tc], W=[Tc])
    P.tcopy("dve", ident, tmpf[:, 0:128], [Tc], [Tc])
    P.emit("pool", lambda e: e.memset(ones, 1.0), R=[Tc], W=[Tc])
    P.emit("pool", lambda e: e.memset(ones_row, 1.0), R=[Tc], W=[Tc])
    P.emit("pool", lambda e: e.memset(epsT, EPS), R=[Tc], W=[Tc])
    P.emit("pool", lambda e: e.memset(tmpf, 1.0), R=[Tc], W=[Tc])
    P.emit("pool", lambda e: e.affine_select(out=tmpf, in_=tmpf, compare_op=ALU.is_ge, fill=0.0, base=0,
                                             pattern=[[1, 256]], channel_multiplier=-1), R=[Tc], W=[Tc])
    P.emit("pool", lambda e: e.affine_select(out=tmpf, in_=tmpf, compare_op=ALU.is_ge, fill=0.0, base=128,
                                             pattern=[[-1, 256]], channel_multiplier=1), R=[Tc], W=[Tc])
    P.tcopy("dve", band, tmpf, [Tc], [Tc])
    ctf = carve(R_LOC + 2 * KB, [128, 16], F32)
    P.dma("sp", ctf, ct_d, W=[Tc], semt=Tc)
    P.act(cact, ctf, AF.Silu, [Tc], [Tc])
    P.barrier()

    def stop(name):
        return stop_after == name

    def finish():
        P.barrier()
        P.build()
        return nc

    def phase_mod(l):
        Tm = T("mod")
        Crep = carve(R_BIG, [128, 16, 128], BF16)
        for kc in range(16):
            P.tcopy("dve", Crep[:, kc, :], cact[:, kc:kc + 1].to_broadcast([128, 128]), [], [Tm], accum=True)
        wv = wada_d[l].rearrange("(kc p) n -> p kc n", p=128)
        Tw = [T("w%d" % i) for i in range(3)]
        Tbb = [T("bb%d" % i) for i in range(2)]
        bbs = [carve(R_LOC + i * 2 * KB, [128, 512], F32) for i in range(2)]
        badT = carve(R_LOC + 4 * KB, [128, 96], F32)
        Tbad = T("badT")
        P.dma("sp", badT, badaT_d[l], W=[Tbad], semt=Tbad)
        n1 = carve(R_LOC + 5 * KB, [128, 16], F32)
        n2 = carve(R_LOC + 5 * KB + 64, [128, 16], F32)
        P.dma("sp", n1, n1_d[l], W=[Tbad], semt=Tbad, accum=True)
        P.dma("sp", n2, n2_d[l], W=[Tbad], semt=Tbad, accum=True)

        def load(ch):
            if ch < 24:
                P.dma("pool", WS[ch % 3], wv[:, :, ch * 512:(ch + 1) * 512], W=[Tw[ch % 3]], semt=Tw[ch % 3])
        load(0)
        load(1)
        mbank, Tmb = banks[7], Tb[7]
        nb = 0
        ng = 0
        for ch in range(24):
            load(ch + 2)
            w, tw = WS[ch % 3], Tw[ch % 3]
            v = ch // 4
            if v in (2, 5):
                bk, tbk = banks[nb % 2], Tb[nb % 2]
                nb += 1
                for kc in range(16):
                    P.mm(bk, Crep[:, kc, :], w[:, kc, :], kc == 0, kc == 15, [tw, Tm], [tbk])
                bb, tbb = bbs[ng % 2], Tbb[ng % 2]
                ng += 1
                P.dma("sp", bb, bada_d[l, ch * 512:(ch + 1) * 512].partition_broadcast(128), W=[tbb], semt=tbb)
                G = G1b if v == 2 else G2b
                c0 = (ch % 4) * 512
                P.tt("dve", G[:, c0:c0 + 512], bk, bb, ALU.add, [tbk, tbb], [Tm], accum=True)
            else:
                vp = {0: 0, 1: 1, 3: 2, 4: 3}[v]
                for cb in range(4):
                    col = vp * 16 + (ch % 4) * 4 + cb
                    for kc in range(16):
                        P.mm(mbank[:, col:col + 1], w[:, kc, cb * 128:(cb + 1) * 128], cact[:, kc:kc + 1],
                             kc == 0, kc == 15, [tw, Tc], [Tmb])
        for vp, v in enumerate((0, 1, 3, 4)):
            P.tt("dve", modT[:, vp * 16:(vp + 1) * 16], mbank[:, vp * 16:(vp + 1) * 16],
                 badT[:, v * 16:(v + 1) * 16], ALU.add, [Tmb, Tbad], [Tm], accum=True)
        P.stt(modT[:, 16:32], modT[:, 16:32], 1.0, n1, ALU.add, ALU.mult, [Tm, Tbad], [Tm])
        P.stt(modT[:, 48:64], modT[:, 48:64], 1.0, n2, ALU.add, ALU.mult, [Tm, Tbad], [Tm])
        if dbg and l == 0:
            P.dma("sp", dbg_mod[:, 0:64], modT, R=[Tm], semt=Tm)
            P.dma("sp", dbg_mod[:, 64:64 + 2048], G1b, R=[Tm], semt=Tm)
            P.dma("sp", dbg_mod[:, 64 + 2048:64 + 4096], G2b, R=[Tm], semt=Tm)
        P.barrier()

    def phase_norm(h_src, Scol, SHcol):
        hb = [carve(R_O + i * 8 * KB, [128, 2048], F32) for i in range(2)]
        Th = [T("h%d" % i) for i in range(2)]
        xh = [carve(R_O + 16 * KB + i * 4 * KB, [128, 2048], BF16) for i in range(4)]
        Tx = [T("xh%d" % i) for i in range(4)]
        junk = carve(R_LOC, [128, 2048], BF16)
        Tj = T("junk")
        st = [carve(R_LOC + 4 * KB + i * 32, [128, 4], F32) for i in range(4)]
        Ts = [T("st%d" % i) for i in range(4)]
        TuT = T("uT")
        trr = Ring(list(range(4)))

        def load(tt):
            if tt < NT:
                P.dma("sp", hb[tt % 2], h_src[tt * 128:(tt + 1) * 128, :], W=[Th[tt % 2]], semt=Th[tt % 2])
        load(0)
        for tq in range(4):
            for t4 in range(4):
                tt = tq * 4 + t4
                load(tt + 1)
                h, th = hb[tt % 2], Th[tt % 2]
                s, tsn = st[t4], Ts[t4]
                P.emit("pool", lambda e, s=s: e.memset(s, 0.0), W=[tsn])
                P.act(junk, h, AF.Square, [th, tsn], [Tj, tsn], accum_out=s[:, 0:1])
                P.act(s[:, 1:2], s[:, 0:1], AF.Sqrt, [tsn], [tsn], scale=1.0 / D, bias=epsT)
                P.emit("dve", lambda e, s=s: e.reciprocal(s[:, 2:3], s[:, 1:2]), R=[tsn], W=[tsn])
                P.ts("dve", xh[t4], h, s[:, 2:3], None, ALU.mult, None, [th, tsn], [Tx[t4]])
            for j in range(16):
                b = trr.next()
                for t4 in range(4):
                    P.tr(banks_bf[b][:, t4 * 128:(t4 + 1) * 128], xh[t4][:, j * 128:(j + 1) * 128], ident,
                         [Tx[t4], Tc], [Tb[b]])
                out = BIG[:, j, tq * 512:(tq + 1) * 512]
                if j % 2 == 0:
                    P.act(out, banks_bf[b][:, 0:512], AF.Identity, [Tb[b]], [TuT], scale=modT[:, Scol + j:Scol + j + 1],
                          bias=modT[:, SHcol + j:SHcol + j + 1], accum=True)
                else:
                    P.ts("dve", out, banks_bf[b][:, 0:512], modT[:, Scol + j:Scol + j + 1],
                         modT[:, SHcol + j:SHcol + j + 1], ALU.mult, ALU.add, [Tb[b]], [TuT], accum=True)
        P.barrier()

    def phase_inproj(l):
        TuT = T("uT")
        Tw = [T("w%d" % i) for i in range(3)]
        wv = win_d[l].rearrange("(kc p) n -> p kc n", p=128)
        tabs = [carve(R_O + i * 8 * KB, [128, 16, 128], F32) for i in range(4)]
        Ttab = T("tabs")
        gq = carve(R_LOC + 20 * KB + 512, [128, 128], F32)
        gk = carve(R_LOC + 21 * KB, [128, 128], F32)
        gqs = carve(R_LOC + 21 * KB + 512, [128, 128], F32)
        gks = carve(R_LOC + 22 * KB, [128, 128], F32)
        Tg = T("gains")
        P.dma("sp", gq, qn_d[l].partition_broadcast(128), W=[Tg], semt=Tg)
        P.dma("sp", gk, kn_d[l].partition_broadcast(128), W=[Tg], semt=Tg, accum=True)
        P.dma("sp", gqs[:, 0:64], qn_d[l, 64:128].partition_broadcast(128), W=[Tg], semt=Tg, accum=True)
        P.dma("sp", gqs[:, 64:128], qn_d[l, 0:64].partition_broadcast(128), W=[Tg], semt=Tg, accum=True)
        P.dma("sp", gks[:, 0:64], kn_d[l, 64:128].partition_broadcast(128), W=[Tg], semt=Tg, accum=True)
        P.dma("sp", gks[:, 64:128], kn_d[l, 0:64].partition_broadcast(128), W=[Tg], semt=Tg, accum=True)
        csv = cs_d.rearrange("(t p) c -> p t c", p=128)
        snv = sn_d.rearrange("(t p) c -> p t c", p=128)
        for i, (src, g) in enumerate(((csv, gq), (snv, gqs), (csv, gk), (snv, gks))):
            P.dma("sp", tabs[i], src, W=[Ttab], semt=Ttab, accum=True)
        for i, g in enumerate((gq, gqs, gk, gks)):
            P.tt("pool", tabs[i], tabs[i], g.unsqueeze(1).to_broadcast([128, 16, 128]), ALU.mult, [Ttab, Tg], [Ttab])

        X = [carve(R_LOC + i * 2 * KB, [128, 4, 128], F32) for i in range(2)]
        t1 = [carve(R_LOC + 4 * KB + i * 2 * KB, [128, 4, 128], F32) for i in range(2)]
        t2 = [carve(R_LOC + 8 * KB + i * 2 * KB, [128, 4, 128], F32) for i in range(2)]
        ob = [carve(R_LOC + 12 * KB + i * KB, [128, 4, 128], BF16) for i in range(3)]
        sT = [carve(R_LOC + 15 * KB + i * KB, [128, 4, 128], BF16) for i in range(2)]
        stg = [carve(R_LOC + 17 * KB + i * KB, [128, 512], BF16) for i in range(3)]
        rs = [carve(R_LOC + 20 * KB + i * 64, [128, 12], F32) for i in range(2)]
        TX = [T("X%d" % i) for i in range(2)]
        Tt1 = [T("t1%d" % i) for i in range(2)]
        Tt2 = [T("t2%d" % i) for i in range(2)]
        Tob = [T("ob%d" % i) for i in range(3)]
        TsT = [T("sT%d" % i) for i in range(2)]
        Tstg = [T("stg%d" % i) for i in range(3)]
        Trs = [T("rs%d" % i) for i in range(2)]
        mmr = Ring(list(range(6)))
        trr = Ring([6, 7])
        rp = [0]
        sg = [0]

        def load(cc):
            if cc < 20:
                P.dma("pool", WS[cc % 3], wv[:, :, cc * 512:(cc + 1) * 512], W=[Tw[cc % 3]], semt=Tw[cc % 3])

        pend = []

        def rope(b, tt, isk, hc, dst):
            n = rp[0]
            i = n % 2
            o = n % 3
            rp[0] += 1
            Xi, t1i, t2i, obi, rsi = X[i], t1[i], t2[i], ob[o], rs[i]
            T1 = tabs[2 * isk][:, tt, :]
            T2 = tabs[2 * isk + 1][:, tt, :]
            bk = banks[b].rearrange("p (a b) -> p a b", a=4)
            P.tcopy("act", Xi, bk, [Tb[b]], [TX[i]])
            P.act(t1i, Xi, AF.Square, [TX[i]], [Tt1[i]])
            P.emit("dve", lambda e: e.tensor_reduce(out=rsi[:, 0:4], in_=t1i, axis=AX.X, op=ALU.add),
                   R=[Tt1[i]], W=[Trs[i]])
            P.act(rsi[:, 4:8], rsi[:, 0:4], AF.Sqrt, [Trs[i]], [Trs[i]], scale=1.0 / HD, bias=epsT)
            P.emit("dve", lambda e: e.reciprocal(rsi[:, 8:12], rsi[:, 4:8]), R=[Trs[i]], W=[Trs[i]])
            P.tt("dve", t1i, Xi, T1.unsqueeze(1).to_broadcast([128, 4, 128]), ALU.mult, [TX[i], Ttab, Trs[i]], [Tt1[i]])
            P.tt("pool", t2i[:, :, 0:64], Xi[:, :, 64:128], T2[:, 0:64].unsqueeze(1).to_broadcast([128, 4, 64]),
                 ALU.mult, [TX[i], Ttab], [Tt2[i]])
            P.tt("pool", t2i[:, :, 64:128], Xi[:, :, 0:64], T2[:, 64:128].unsqueeze(1).to_broadcast([128, 4, 64]),
                 ALU.mult, [TX[i], Ttab], [Tt2[i]], accum=True)
            P.tt("pool", t1i, t1i, t2i, ALU.add, [Tt1[i], Tt2[i]], [Tt1[i]])
            P.tt("dve", obi, t1i, rsi[:, 8:12].unsqueeze(2).to_broadcast([128, 4, 128]), ALU.mult,
                 [Tt1[i], Trs[i]], [Tob[o]])
            pend.append((n, tt, hc, dst))
            if len(pend) > 2:
                rope2(*pend.pop(0))

        def rope2(n, tt, hc, dst):
            o = n % 3
            i = n % 2
            obi, sTi = ob[o], sT[i]
            tb = trr.next()
            for hh in range(4):
                P.tr(banks_bf[tb][:, hh * 128:(hh + 1) * 128], obi[:, hh, :], ident, [Tob[o], Tc], [Tb[tb]])
            P.evac(sTi, banks_bf[tb][:, 0:512].rearrange("p (a b) -> p a b", a=4), [Tb[tb]], [TsT[i]])
            P.dma("sp", dst[hc * 4:(hc + 1) * 4, :, tt * 128:(tt + 1) * 128].rearrange("h p t -> p h t"), sTi,
                  R=[TsT[i]], semt=TsT[i])

        load(0)
        load(1)
        for cc in range(20):
            load(cc + 2)
            w, tw = WS[cc % 3], Tw[cc % 3]
            if cc < 6:
                isk = cc // 3
                for tt in range(NT):
                    b = mmr.next()
                    for kc in range(16):
                        P.mm(banks[b], BIG[:, kc, tt * 128:(tt + 1) * 128], w[:, kc, :], kc == 0, kc == 15,
                             [tw, TuT], [Tb[b]])
                    rope(b, tt, isk, cc % 3, kTA_d if isk else qTA_d)
            elif cc < 9:
                while pend:
                    rope2(*pend.pop(0))
                g = cc - 6
                d = (1, 4, 16)[g]
                nbk = 16 // d
                for idx in range(16):
                    r, n = idx // nbk, idx % nbk
                    st0 = r + d * 128 * n
                    b = mmr.next()
                    for kc in range(16):
                        P.mm(banks[b], BIG[:, kc, st0:st0 + d * 127 + 1:d], w[:, kc, :], kc == 0, kc == 15,
                             [tw, TuT], [Tb[b]])
                    i = sg[0] % 3
                    sg[0] += 1
                    P.evac(stg[i], banks[b], [Tb[b]], [Tstg[i]])
                    P.dma("sp", vA_d[idx, :, g * 512:(g + 1) * 512], stg[i], R=[Tstg[i]], semt=Tstg[i])
            elif cc == 11:
                for tt in range(NT):
                    b = mmr.next()
                    for kc in range(16):
                        P.mm(banks[b], BIG[:, kc, tt * 128:(tt + 1) * 128], w[:, kc, :], kc == 0, kc == 15,
                             [tw, TuT], [Tb[b]])
                    i = sg[0] % 3
                    sg[0] += 1
                    P.evac(stg[i], banks[b], [Tb[b]], [Tstg[i]])
                    P.dma("sp", vB_d[tt], stg[i], R=[Tstg[i]], semt=Tstg[i])
            else:
                for cb in range(4):
                    if cc == 9:
                        dst, sig = qTB_d[cb], False
                    elif cc == 10:
                        dst, sig = kTB_d[cb], False
                    elif cc < 16:
                        dst, sig = gaT_d[(cc - 12) * 4 + cb], True
                    else:
                        dst, sig = gbT_d[(cc - 16) * 4 + cb], True
                    for tq in range(4):
                        b = mmr.next()
                        for kc in range(16):
                            P.mm(banks[b], w[:, kc, cb * 128:(cb + 1) * 128], BIG[:, kc, tq * 512:(tq + 1) * 512],
                                 kc == 0, kc == 15, [tw, TuT], [Tb[b]])
                        i = sg[0] % 3
                        sg[0] += 1
                        if sig:
                            P.act(stg[i], banks[b], AF.Sigmoid, [Tb[b]], [Tstg[i]])
                        else:
                            P.evac(stg[i], banks[b], [Tb[b]], [Tstg[i]])
                        P.dma("sp", dst[:, tq * 512:(tq + 1) * 512], stg[i], R=[Tstg[i]], semt=Tstg[i])
        P.barrier()

    OAT = carve(R_O, [128, 4, 2048], BF16)
    OBT = carve(R_O + 16 * KB, [128, 4, 2048], BF16)

    def phase_mixer_a():
        numacc = carve(R_BIG, [128, 4, 2048], F32)
        denacc = carve(R_BIG + 32 * KB, [128, 4, 2048], F32)
        Tacc = [T("acc%d" % j) for j in range(4)]
        Tdacc = [T("dacc%d" % j) for j in range(4)]
        for j in range(4):
            P.emit("pool", lambda e, j=j: e.memset(numacc[:, j, :], 0.0), W=[Tacc[j]])
            P.emit("pool", lambda e, j=j: e.memset(denacc[:, j, :], 0.0), W=[Tdacc[j]])
        Vb = [carve(R_WS + i * 16 * KB, [128, 16, 512], BF16) for i in range(2)]
        TV = [T("V%d" % i) for i in range(2)]
        QK = [carve(R_WS + 32 * KB + i * 4 * KB, [128, 2048], BF16) for i in range(4)]
        TQK = [T("QK%d" % i) for i in range(4)]
        pT = [carve(R_LOC + i * 512, [128, 256], BF16) for i in range(4)]
        TpT = [T("pT%d" % i) for i in range(4)]
        rec = carve(R_LOC + 4 * KB, [128, 2048], F32)
        Trec = T("rec")
        sring = Ring([0, 1, 2, 3])
        slots = [(4 + s // 4, 6 + s // 4, (s % 4) * 128) for s in range(8)]
        Tns = [T("ns%d" % s) for s in range(8)]
        Tds = [T("ds%d" % s) for s in range(8)]
        sc = [0]
        pc = [0]
        loads = []
        for g in range(3):
            for j in range(4):
                loads.append((g, j))

        def load(i):
            if i < len(loads):
                g, j = loads[i]
                head = g * 4 + j
                if j == 0:
                    P.dma("sp", Vb[g % 2], vA_d[:, :, g * 512:(g + 1) * 512].rearrange("t p c -> p t c"),
                          W=[TV[g % 2]], semt=TV[g % 2])
                q = (i % 2) * 2
                P.dma("sp", QK[q], qTA_d[head], W=[TQK[q]], semt=TQK[q])
                P.dma("sp", QK[q + 1], kTA_d[head], W=[TQK[q + 1]], semt=TQK[q + 1])
        load(0)
        units = []
        for i, (g, j) in enumerate(loads):
            d = (1, 4, 16)[g]
            nbk = 16 // d
            for r in range(d):
                for kt in range(nbk):
                    units.append((i, g, j, d, nbk, r, kt))
        state = {"cur": None}

        def stage1(u, pi):
            i, g, j, d, nbk, r, kt = u
            q = (i % 2) * 2
            QT, KT, tq_, tk_ = QK[q], QK[q + 1], TQK[q], TQK[q + 1]
            nq = 256 if kt < nbk - 1 else 128
            k0 = r + d * 128 * kt
            sb = sring.next()
            P.mm(banks[sb][:, 0:nq], KT[:, k0:k0 + d * 127 + 1:d], QT[:, k0:k0 + d * (nq - 1) + 1:d],
                 True, True, [tq_, tk_], [Tb[sb]])
            P.act(pT[pi][:, 0:nq], banks[sb][:, 0:nq], AF.Exp, [Tb[sb]], [TpT[pi]], scale=SCALE)
            P.tt("dve", pT[pi][:, 0:nq], pT[pi][:, 0:nq], band[:, 0:nq], ALU.mult, [TpT[pi], Tc], [TpT[pi]])

        def stage2(u, pi):
            i, g, j, d, nbk, r, kt = u
            V, tv = Vb[g % 2], TV[g % 2]
            nq = 256 if kt < nbk - 1 else 128
            k0 = r + d * 128 * kt
            Vl = V[:, r * nbk + kt, j * 128:(j + 1) * 128]
            if kt == 0:
                cur = sc[0] % 8
                sc[0] += 1
                first = True
            else:
                cur = state["cur"]
                first = False
            nbank, dbank, c0 = slots[cur]
            P.mm(banks[nbank][:, c0:c0 + 128], Vl, pT[pi][:, 0:128], first, True, [tv, TpT[pi]], [Tns[cur]])
            P.mm(banks[dbank][:, c0:c0 + 128], ones, pT[pi][:, 0:128], first, True, [Tc, TpT[pi]], [Tds[cur]])
            tok = slice(k0, k0 + d * 127 + 1, d)
            P.tt("dve", numacc[:, j, tok], numacc[:, j, tok], banks[nbank][:, c0:c0 + 128], ALU.add,
                 [Tns[cur], Tacc[j]], [Tacc[j]])
            P.tt("dve", denacc[:, j, tok], denacc[:, j, tok], banks[dbank][:, c0:c0 + 128], ALU.add,
                 [Tds[cur], Tdacc[j]], [Tdacc[j]])
            if nq == 256:
                nxt = sc[0] % 8
                sc[0] += 1
                nbank2, dbank2, c2 = slots[nxt]
                P.mm(banks[nbank2][:, c2:c2 + 128], Vl, pT[pi][:, 128:256], True, False,
                     [tv, TpT[pi]], [Tns[nxt]])
                P.mm(banks[dbank2][:, c2:c2 + 128], ones, pT[pi][:, 128:256], True, False,
                     [Tc, TpT[pi]], [Tds[nxt]])
                state["cur"] = nxt
            else:
                state["cur"] = None

        LAG = 0
        lastload = -1
        for n, u in enumerate(units):
            if u[0] != lastload:
                lastload = u[0]
                load(u[0] + 1)
            stage1(u, n % 4)
            if n >= LAG:
                stage2(units[n - LAG], (n - LAG) % 4)
        for n in range(max(0, len(units) - LAG), len(units)):
            stage2(units[n], n % 4)
        for j in range(4):
            P.emit("dve", lambda e, j=j: e.reciprocal(rec, denacc[:, j, :]), R=[Tdacc[j]], W=[Trec])
            P.tt("dve", OAT[:, j, :], numacc[:, j, :], rec, ALU.mult, [Tacc[j], Trec], [Tdacc[j]])
        if dbg:
            for j in range(4):
                P.dma("sp", dbg_oT[j], OAT[:, j, :], R=[Tdacc[j]], semt=Tdacc[j])
        P.barrier()

    def phase_mixer_b():
        fb = [[carve(R_BIG + (s * 4 + i) * 8 * KB, [128, 2048], F32) for i in range(4)] for s in range(2)]
        Tfb = [[T("fb%d_%d" % (s, i)) for i in range(4)] for s in range(2)]
        QKV = [[carve(R_WS + (s * 3 + i) * 4 * KB, [128, 2048], BF16) for i in range(3)] for s in range(2)]
        TQKV = [[T("qkv%d_%d" % (s, i)) for i in range(3)] for s in range(2)]
        Ab = [carve(R_WS + 24 * KB + i * 4 * KB, [128, 2048], BF16) for i in range(2)]
        TA = [T("A%d" % i) for i in range(2)]
        ATb = [carve(R_WS + 32 * KB + i * 4 * KB, [128, 16, 128], BF16) for i in range(2)]
        TAT = [T("AT%d" % i) for i in range(2)]
        TOB = T("OBT")
        zr = Ring([0, 1, 2, 3])
        trr = Ring([4, 5])
        orr = Ring([6, 7])

        def load(hh):
            if hh < 4:
                s = hh % 2
                P.dma("sp", QKV[s][0], qTB_d[hh], W=[TQKV[s][0]], semt=TQKV[s][0])
                P.dma("sp", QKV[s][1], kTB_d[hh], W=[TQKV[s][1]], semt=TQKV[s][1])
                P.dma("sp", QKV[s][2].rearrange("p (t c) -> p t c", t=16),
                      vB_d[:, :, hh * 128:(hh + 1) * 128].rearrange("t p c -> p t c"),
                      W=[TQKV[s][2]], semt=TQKV[s][2])
        load(0)

        def stage1(hh, qt, bs):
            s = hh % 2
            QT, KT = QKV[s][0], QKV[s][1]
            tq_, tk_, tv_ = TQKV[s]
            nk = 128 * (qt + 1)
            E, SPt, LB, G = fb[bs]
            tE, tSP, tLB, tG = Tfb[bs]
            A, tA = Ab[bs], TA[bs]
            nch = (nk + 511) // 512
            for c in range(nch):
                w = min(512, nk - c * 512)
                cs_ = slice(c * 512, c * 512 + w)
                zb = zr.next()
                P.mm(banks[zb][:, 0:w], QT[:, qt * 128:(qt + 1) * 128], KT[:, cs_], True, True,
                     [tq_, tk_], [Tb[zb]])
                P.act(E[:, cs_], banks[zb][:, 0:w], AF.Exp, [Tb[zb]], [tE], scale=SCALE, accum=(c > 0))
                P.act(SPt[:, cs_], E[:, cs_], AF.Ln, [tE], [tSP], bias=1.0, accum=(c > 0))
                P.stt(LB[:, cs_], banks[zb][:, 0:w], SCALE, SPt[:, cs_], ALU.mult, ALU.subtract,
                      [Tb[zb], tSP], [tLB], accum=(c > 0))
            dg = slice(nk - 128, nk)
            P.emit("pool", lambda e: e.affine_select(
                out=SPt[:, dg], in_=SPt[:, dg], compare_op=ALU.is_ge, fill=0.0, base=-1,
                pattern=[[-1, 128]], channel_multiplier=1), R=[tSP, tLB], W=[tSP])
            P.emit("dve", lambda e: e.tensor_tensor_scan(
                out=G[:, 0:nk], data0=ones_row[:, 0:nk], data1=SPt[:, 0:nk], initial=0.0,
                op0=ALU.mult, op1=ALU.add), R=[tSP, Tc], W=[tG])
            P.stt(LB[:, 0:nk], G[:, 0:nk], G[:, nk - 1:nk], LB[:, 0:nk], ALU.subtract, ALU.add,
                  [tG, tLB], [tLB])
            P.act(A[:, 0:nk], LB[:, 0:nk], AF.Exp, [tLB], [tA])
            P.emit("pool", lambda e: e.affine_select(
                out=A[:, dg], in_=A[:, dg], compare_op=ALU.is_ge, fill=0.0, base=-1,
                pattern=[[-1, 128]], channel_multiplier=1), R=[tA], W=[tA])

        def stage2(hh, qt, bs):
            s = hh % 2
            V = QKV[s][2].rearrange("p (t c) -> p t c", t=16)
            tq_, tk_, tv_ = TQKV[s]
            A, tA = Ab[bs], TA[bs]
            AT, tAT = ATb[bs], TAT[bs]
            nblk = qt + 1
            for g0 in range(0, nblk, 4):
                gn = min(4, nblk - g0)
                tb = trr.next()
                for i in range(gn):
                    kb = g0 + i
                    P.tr(banks_bf[tb][:, i * 128:(i + 1) * 128], A[:, kb * 128:(kb + 1) * 128], ident,
                         [tA, Tc], [Tb[tb]])
                P.evac(AT[:, g0:g0 + gn, :], banks_bf[tb][:, 0:gn * 128].rearrange("p (a b) -> p a b", a=gn),
                       [Tb[tb]], [tAT], accum=(g0 > 0))
            ob_ = orr.next()
            for kb in range(nblk):
                P.mm(banks[ob_][:, 0:128], V[:, kb, :], AT[:, kb, :], kb == 0, kb == nblk - 1,
                     [tv_, tAT], [Tb[ob_]])
            P.evac(OBT[:, hh, qt * 128:(qt + 1) * 128], banks[ob_][:, 0:128], [Tb[ob_]], [TOB], accum=True)

        units = [(hh, qt) for hh in range(4) for qt in range(NT)]
        prev = None
        for n, (hh, qt) in enumerate(units):
            stage1(hh, qt, n % 2)
            if prev is not None:
                stage2(*prev)
            prev = (hh, qt, n % 2)
            if qt == 0:
                load(hh + 1)
        stage2(*prev)
        if dbg:
            for j in range(4):
                P.dma("sp", dbg_oT[4 + j], OBT[:, j, :], R=[TOB], semt=TOB)
        P.barrier()

    def phase_merge(l):
        MT = BIG
        TMT = T("MT")
        wba = carve(R_WS, [128, 4, 2048], BF16)
        wbb = carve(R_WS + 16 * KB, [128, 4, 2048], BF16)
        Twb = T("wb")
        To = T("o")
        P.dma("pool", wba, wba_d[l].rearrange("(k p) n -> p k n", p=128), W=[Twb], semt=Twb)
        P.dma("pool", wbb, wbb_d[l].rearrange("(k p) n -> p k n", p=128), W=[Twb], semt=Twb, accum=True)
        sg = [carve(R_WS + 32 * KB + i * 4 * KB, [128, 2048], BF16) for i in range(4)]
        Tsg = [T("sg%d" % i) for i in range(4)]
        m1 = [carve(R_LOC + i * 2 * KB, [128, 512], F32) for i in range(2)]
        m2 = [carve(R_LOC + 4 * KB + i * 2 * KB, [128, 512], F32) for i in range(2)]
        Tm1 = [T("m1%d" % i) for i in range(2)]
        Tm2 = [T("m2%d" % i) for i in range(2)]
        br = Ring([0, 1, 2, 3, 4, 5, 6, 7])

        def load(fc):
            if fc < 16:
                s = (fc % 2) * 2
                P.dma("sp", sg[s], gaT_d[fc], W=[Tsg[s]], semt=Tsg[s])
                P.dma("sp", sg[s + 1], gbT_d[fc], W=[Tsg[s + 1]], semt=Tsg[s + 1])
        load(0)
        it = 0
        for fc in range(16):
            load(fc + 1)
            s = (fc % 2) * 2
            for tq in range(4):
                ts_ = slice(tq * 512, (tq + 1) * 512)
                ya = br.next()
                for j in range(4):
                    P.mm(banks[ya], wba[:, j, fc * 128:(fc + 1) * 128], OAT[:, j, ts_], j == 0, j == 3, [Twb, To], [Tb[ya]])
                yb = br.next()
                for j in range(4):
                    P.mm(banks[yb], wbb[:, j, fc * 128:(fc + 1) * 128], OBT[:, j, ts_], j == 0, j == 3, [Twb, To], [Tb[yb]])
                i = it % 2
                it += 1
                P.tt("dve", m1[i], banks[ya], sg[s][:, ts_], ALU.mult, [Tb[ya], Tsg[s]], [Tm1[i]])
                P.tt("dve", m2[i], banks[yb], sg[s + 1][:, ts_], ALU.mult, [Tb[yb], Tsg[s + 1]], [Tm2[i]])
                P.tt("pool", MT[:, fc, ts_], m1[i], m2[i], ALU.add, [Tm1[i], Tm2[i]], [TMT], accum=True)
        P.barrier()

    def resid_update(bank, tbank, Gb, h_src, tt, c, hp, thp, tmp, ttmp):
        rows = slice(tt * 128, (tt + 1) * 128)
        cols = slice(c * 512, (c + 1) * 512)
        P.dma("sp", hp, h_src[rows, cols], W=[thp], semt=thp)
        P.tt("dve", tmp, bank, Gb[:, cols], ALU.mult, [tbank], [ttmp])
        P.tt("pool", hp, hp, tmp, ALU.add, [thp, ttmp], [thp])
        P.dma("sp", y_d[rows, cols], hp, R=[thp], semt=thp)

    def phase_outproj(l, h_src):
        MT = BIG
        TMT = T("MT")
        Tw = [T("w%d" % i) for i in range(3)]
        wv = wout_d[l].rearrange("(kc p) n -> p kc n", p=128)
        hp = [carve(R_LOC + i * 2 * KB, [128, 512], F32) for i in range(4)]
        Thp = [T("hp%d" % i) for i in range(4)]
        tmp = [carve(R_LOC + 8 * KB + i * 2 * KB, [128, 512], F32) for i in range(2)]
        Ttmp = [T("tmp%d" % i) for i in range(2)]
        br = Ring([0, 1, 2, 3, 4, 5, 6, 7])

        def load(c):
            if c < 4:
                P.dma("pool", WS[c % 3], wv[:, :, c * 512:(c + 1) * 512], W=[Tw[c % 3]], semt=Tw[c % 3])
        load(0)
        load(1)
        it = 0
        for c in range(4):
            load(c + 2)
            w, tw = WS[c % 3], Tw[c % 3]
            for tt in range(NT):
                b = br.next()
                for fc in range(16):
                    P.mm(banks[b], MT[:, fc, tt * 128:(tt + 1) * 128], w[:, fc, :], fc == 0, fc == 15, [tw, TMT], [Tb[b]])
                resid_update(banks[b], Tb[b], G1b, h_src, tt, c, hp[it % 4], Thp[it % 4], tmp[it % 2], Ttmp[it % 2])
                it += 1
        P.barrier()

    def phase_gateup(l):
        TuT = T("uT")
        Tw = [T("w%d" % i) for i in range(3)]
        wv = wgu_d[l].rearrange("(kc p) n -> p kc n", p=128)
        sgl = [carve(R_LOC + i * 2 * KB, [128, 512], F32) for i in range(2)]
        Tsgl = [T("sgl%d" % i) for i in range(2)]
        ast = [carve(R_LOC + 4 * KB + i * KB, [128, 512], BF16) for i in range(3)]
        Tast = [T("ast%d" % i) for i in range(3)]
        br = Ring([0, 1, 2, 3, 4, 5, 6, 7])

        def load(st):
            if st < 22:
                i = st % 3
                P.dma("pool", WS[i][:, :, 0:256], wv[:, :, st * 256:(st + 1) * 256], W=[Tw[i]], semt=Tw[i])
                P.dma("pool", WS[i][:, :, 256:512], wv[:, :, DFF + st * 256:DFF + (st + 1) * 256], W=[Tw[i]],
                      semt=Tw[i], accum=True)
        load(0)
        load(1)
        it = 0
        for st in range(22):
            load(st + 2)
            w, tw = WS[st % 3], Tw[st % 3]
            for fb2 in range(2):
                fbi = st * 2 + fb2
                for tq in range(4):
                    ts_ = slice(tq * 512, (tq + 1) * 512)
                    gb_ = br.next()
                    for kc in range(16):
                        P.mm(banks[gb_], w[:, kc, fb2 * 128:(fb2 + 1) * 128], BIG[:, kc, ts_], kc == 0, kc == 15,
                             [tw, TuT], [Tb[gb_]])
                    ub_ = br.next()
                    for kc in range(16):
                        P.mm(banks[ub_], w[:, kc, 256 + fb2 * 128:256 + (fb2 + 1) * 128], BIG[:, kc, ts_], kc == 0,
                             kc == 15, [tw, TuT], [Tb[ub_]])
                    i = it % 2
                    k = it % 3
                    it += 1
                    P.act(sgl[i], banks[gb_], AF.Silu, [Tb[gb_]], [Tsgl[i]])
                    P.tt("dve", ast[k], sgl[i], banks[ub_], ALU.mult, [Tsgl[i], Tb[ub_]], [Tast[k]])
                    P.dma("sp", aT_d[fbi, :, ts_], ast[k], R=[Tast[k]], semt=Tast[k])
        P.barrier()

    def phase_down(l):
        Wd = [carve(R_BIG, [128, 44, 512], BF16), carve(R_WS, [128, 44, 512], BF16)]
        TWd = [T("Wd0"), T("Wd1")]
        wv = wdn_d[l].rearrange("(fb p) n -> p fb n", p=128)
        at = [carve(R_O, [128, 11, 512], BF16), carve(R_O + 11 * KB, [128, 11, 512], BF16),
              carve(R_BIG + 44 * KB, [128, 11, 512], BF16)]
        Tat = [T("at%d" % i) for i in range(3)]
        hp = [carve(R_LOC + i * 2 * KB, [128, 512], F32) for i in range(4)]
        Thp = [T("hp%d" % i) for i in range(4)]
        tmp = [carve(R_LOC + 8 * KB + i * 2 * KB, [128, 512], F32) for i in range(2)]
        Ttmp = [T("tmp%d" % i) for i in range(2)]
        atl = [(c, tq, kg) for c in range(4) for tq in range(4) for kg in range(4)]

        def loadw(c):
            if c < 4:
                for kg in range(4):
                    P.dma("pool", Wd[c % 2][:, kg * 11:(kg + 1) * 11, :], wv[:, kg * 11:(kg + 1) * 11, c * 512:(c + 1) * 512],
                          W=[TWd[c % 2]], semt=TWd[c % 2], accum=(kg > 0))

        def loada(i):
            if i < len(atl):
                c, tq, kg = atl[i]
                P.dma("sp", at[i % 3], aT_d[kg * 11:(kg + 1) * 11, :, tq * 512:(tq + 1) * 512].rearrange("f p t -> p f t"),
                      W=[Tat[i % 3]], semt=Tat[i % 3])
        loadw(0)
        loada(0)
        loada(1)
        ai = 0
        it = 0
        grp = 0
        for c in range(4):
            loadw(c + 1)
            W_, tW = Wd[c % 2], TWd[c % 2]
            for tq in range(4):
                bset = [0, 1, 2, 3] if grp % 2 == 0 else [4, 5, 6, 7]
                grp += 1
                for kg in range(4):
                    loada(ai + 2)
                    a_, ta = at[ai % 3], Tat[ai % 3]
                    ai += 1
                    for t4 in range(4):
                        b = bset[t4]
                        for f in range(11):
                            P.mm(banks[b], a_[:, f, t4 * 128:(t4 + 1) * 128], W_[:, kg * 11 + f, :],
                                 kg == 0 and f == 0, kg == 3 and f == 10, [ta, tW], [Tb[b]])
                for t4 in range(4):
                    b = bset[t4]
                    resid_update(banks[b], Tb[b], G2b, y_d, tq * 4 + t4, c, hp[it % 4], Thp[it % 4], tmp[it % 2], Ttmp[it % 2])
                    it += 1
        P.barrier()

    for l in range(nlayers):
        phase_mod(l)
        if stop("mod"):
            return finish()
        phase_norm(x_d if l == 0 else y_d, 16, 0)
        if dbg and l == 0:
            Td = T("dbg")
            for j in range(16):
                P.dma("sp", dbg_uT[j], BIG[:, j, :], semt=Td)
            P.barrier()
        if stop("norm"):
            return finish()
        phase_inproj(l)
        if stop("inproj"):
            return finish()
        phase_mixer_a()
        if stop("mixa"):
            return finish()
        phase_mixer_b()
        if stop("mixb"):
            return finish()
        phase_merge(l)
        phase_outproj(l, x_d if l == 0 else y_d)
        if stop("outproj"):
            return finish()
        phase_norm(y_d, 48, 32)
        phase_gateup(l)
        phase_down(l)
    return finish()


def host_inputs(inputs):
    f = lambda a: np.ascontiguousarray(np.asarray(a, dtype=np.float32))
    x = f(inputs["x"])
    c = f(inputs["c"])
    B = x.shape[0]
    inv = np.power(np.float32(10000.0), -np.arange(0, HD, 2, dtype=np.float32) / np.float32(HD)).astype(np.float32)
    ang = (np.arange(S, dtype=np.float32)[:, None] * inv[None, :]).astype(np.float32)
    cos = np.cos(ang.astype(np.float64)).astype(np.float32)
    sin = np.sin(ang.astype(np.float64)).astype(np.float32)
    shared = {
        "w_ada": f(inputs["w_ada"]),
        "b_ada": f(inputs["b_ada"]),
        "b_adaT": np.ascontiguousarray(f(inputs["b_ada"]).reshape(NL, 96, 128).transpose(0, 2, 1)),
        "n1T": np.ascontiguousarray(f(inputs["norm1_g"]).reshape(NL, 16, 128).transpose(0, 2, 1)),
        "n2T": np.ascontiguousarray(f(inputs["norm2_g"]).reshape(NL, 16, 128).transpose(0, 2, 1)),
        "w_in": f(inputs["w_in"]),
        "qn_g": f(inputs["qn_g"]),
        "kn_g": f(inputs["kn_g"]),
        "w_branch_a": f(inputs["w_branch_a"]),
        "w_branch_b": f(inputs["w_branch_b"]),
        "w_out": f(inputs["w_out"]),
        "w_gate_up": f(inputs["w_gate_up"]),
        "w_down": f(inputs["w_down"]),
        "rope_cs": np.ascontiguousarray(np.concatenate([cos, cos], axis=1)),
        "rope_sn": np.ascontiguousarray(np.concatenate([-sin, sin], axis=1)),
    }
    in_maps = []
    for b in range(B):
        m = dict(shared)
        m["x"] = x[b]
        m["ct"] = np.ascontiguousarray(c[b].reshape(16, 128).T)
        in_maps.append(m)
    return in_maps


def kernel(**inputs):
    in_maps = host_inputs(inputs)
    nc = build_program()
    res = run_bass_kernel_spmd(nc, in_maps, core_ids=list(range(len(in_maps))))
    return np.stack([np.asarray(r["y"], dtype=np.float32) for r in res.results], axis=0)
```

```python
import contextlib
import numpy as np
import ml_dtypes
import concourse.bass as bass
import concourse.mybir as mybir
from concourse.bass_utils import run_bass_kernel_spmd

F32 = mybir.dt.float32
BF16 = mybir.dt.bfloat16
U8 = mybir.dt.uint8
AF = mybir.ActivationFunctionType
ALU = mybir.AluOpType
AX = mybir.AxisListType

S = 2048
D = 2048
NT = 16
HD = 128
DFF = 5632
NL = 2
INW = 10240
EPS = 1e-6
ENGS = ("pe", "act", "dve", "pool", "sp")
KB = 1024


class T:
    __slots__ = ("name", "w", "r", "dsem")

    def __init__(self, name):
        self.name = name
        self.w = []
        self.r = []
        self.dsem = None


class Prog:
    def __init__(self, nc):
        self.nc = nc
        self.q = {e: [] for e in ENGS}
        self.cnt = {}
        self.seen = {e: {} for e in ENGS}
        self.sem_keys = []
        for e in ("pe", "act", "dve", "pool"):
            self._newsem("E_" + e)
        self.dfree = []
        self.dused = []
        self.flip = 0

    def _newsem(self, key):
        self.cnt[key] = 0
        self.sem_keys.append(key)
        return key

    def dsem(self, t):
        if t.dsem is None:
            if self.dfree:
                t.dsem = self.dfree.pop()
            else:
                t.dsem = self._newsem("D%d" % len(self.sem_keys))
            self.dused.append(t.dsem)
        return t.dsem

    def _waits(self, eng, R, W):
        ws = {}
        seen = self.seen[eng]

        def add(tok):
            k, v = tok
            if eng == "pe" and k == "E_pe":
                return
            if seen.get(k, 0) >= v:
                return
            if ws.get(k, 0) < v:
                ws[k] = v
        for t in R:
            for tok in t.w:
                add(tok)
        for t in W:
            for tok in t.w:
                add(tok)
            for tok in t.r:
                add(tok)
        for k, v in ws.items():
            seen[k] = v
        return list(ws.items())

    def _post(self, tok, R, W, accum):
        for t in R:
            t.r.append(tok)
        for t in W:
            if accum:
                t.w.append(tok)
            else:
                t.w = [tok]
                t.r = []

    def emit(self, eng, fn, R=(), W=(), accum=False):
        waits = self._waits(eng, R, W)
        key = "E_" + eng
        self.cnt[key] += 1
        tok = (key, self.cnt[key])
        self.q[eng].append((waits, fn, (key, self.cnt[key])))
        self._post(tok, R, W, accum)
        return tok

    def dma(self, queue, out, in_, R=(), W=(), semt=None, accum=False):
        waits = self._waits(queue, R, W)
        key = self.dsem(semt)
        self.cnt[key] += 16
        tok = (key, self.cnt[key])
        self.q[queue].append((waits, lambda e: e.dma_start(out=out, in_=in_), (key, 16)))
        self._post(tok, R, W, accum)
        return tok

    def barrier(self):
        for eng in ENGS:
            ws = []
            for k in self.sem_keys:
                v = self.cnt[k]
                if v > self.seen[eng].get(k, 0):
                    ws.append((k, v))
                    self.seen[eng][k] = v
            if ws:
                self.q[eng].append((ws, None, None))
        self.dfree.extend(self.dused)
        self.dused = []

    def mm(self, out, lhsT, rhs, start, stop, R, W):
        return self.emit("pe", lambda e: e.matmul(out, lhsT=lhsT, rhs=rhs, start=start, stop=stop), R, W)

    def tr(self, out, in_, ident, R, W):
        return self.emit("pe", lambda e: e.transpose(out, in_, ident), R, W)

    def act(self, out, in_, func, R, W, scale=1.0, bias=None, accum_out=None, accum=False):
        def f(e):
            kw = {}
            if bias is not None:
                kw["bias"] = bias
            if accum_out is not None:
                kw["accum_out"] = accum_out
            return e.activation(out=out, in_=in_, func=func, scale=scale, **kw)
        return self.emit("act", f, R, W, accum)

    def tcopy(self, eng, out, in_, R, W, accum=False):
        if eng == "act":
            return self.emit("act", lambda e: e.copy(out, in_), R, W, accum)
        return self.emit(eng, lambda e: e.tensor_copy(out, in_), R, W, accum)

    def tt(self, eng, out, in0, in1, op, R, W, accum=False):
        return self.emit(eng, lambda e: e.tensor_tensor(out, in0, in1, op), R, W, accum)

    def ts(self, eng, out, in0, s1, s2, op0, op1, R, W, accum=False):
        if s2 is None:
            return self.emit(eng, lambda e: e.tensor_scalar(out, in0, s1, None, op0), R, W, accum)
        return self.emit(eng, lambda e: e.tensor_scalar(out, in0, s1, s2, op0, op1), R, W, accum)

    def stt(self, out, in0, scalar, in1, op0, op1, R, W, accum=False):
        return self.emit("dve", lambda e: e.scalar_tensor_tensor(out, in0, scalar, in1, op0, op1), R, W, accum)

    def evac(self, out, in_, R, W, accum=False):
        self.flip ^= 1
        return self.tcopy("act" if self.flip else "dve", out, in_, R, W, accum)

    def build(self):
        nc = self.nc
        flagged = {k: set() for k in self.sem_keys if k.startswith("E_")}
        for e in ENGS:
            for waits, fn, inc in self.q[e]:
                for k, v in waits:
                    if k in flagged:
                        flagged[k].add(v)
        rank = {}
        for k, st in flagged.items():
            rank[k] = {v: i + 1 for i, v in enumerate(sorted(st))}
        with contextlib.ExitStack() as es:
            sems = {}
            for k in self.sem_keys:
                sems[k] = es.enter_context(nc.semaphore(k))
            block = es.enter_context(nc.Block())
            prog = self

            def run(engname):
                def body(eng):
                    for waits, fn, inc in prog.q[engname]:
                        for k, v in waits:
                            if k in rank:
                                v = rank[k][v]
                            eng.wait_ge(sems[k], v)
                        if fn is not None:
                            ins = fn(eng)
                            if inc is not None:
                                k, n = inc
                                if k in rank:
                                    if n in rank[k]:
                                        ins.then_inc(sems[k], 1)
                                else:
                                    ins.then_inc(sems[k], n)
                return body
            block.tensor(run("pe"))
            block.scalar(run("act"))
            block.vector(run("dve"))
            block.gpsimd(run("pool"))
            block.sync(run("sp"))


class Ring:
    def __init__(self, items):
        self.items = items
        self.i = 0

    def next(self):
        it = self.items[self.i % len(self.items)]
        self.i += 1
        return it


def _prod(xs):
    p = 1
    for x in xs:
        p *= x
    return p


def build_program(nlayers=NL, dbg=False, stop_after=None):
    nc = bass.Bass("TRN2", target_bir_lowering=False)
    P = Prog(nc)

    def din(name, shape, dt=F32):
        return nc.dram_tensor(name, list(shape), dt, kind="ExternalInput").ap()

    skind = "ExternalOutput" if dbg else "Internal"

    def dscr(name, shape, dt=BF16):
        return nc.dram_tensor(name, list(shape), dt, kind=skind).ap()

    x_d = din("x", [S, D])
    ct_d = din("ct", [128, 16])
    wada_d = din("w_ada", [NL, D, 6 * D])
    bada_d = din("b_ada", [NL, 6 * D])
    badaT_d = din("b_adaT", [NL, 128, 96])
    n1_d = din("n1T", [NL, 128, 16])
    n2_d = din("n2T", [NL, 128, 16])
    win_d = din("w_in", [NL, D, INW])
    qn_d = din("qn_g", [NL, 128])
    kn_d = din("kn_g", [NL, 128])
    wba_d = din("w_branch_a", [NL, 512, D])
    wbb_d = din("w_branch_b", [NL, 512, D])
    wout_d = din("w_out", [NL, D, D])
    wgu_d = din("w_gate_up", [NL, D, 2 * DFF])
    wdn_d = din("w_down", [NL, DFF, D])
    cs_d = din("rope_cs", [S, 128])
    sn_d = din("rope_sn", [S, 128])
    y_d = nc.dram_tensor("y", [S, D], F32, kind="ExternalOutput").ap()

    qTA_d = dscr("s_qTA", [12, 128, S])
    kTA_d = dscr("s_kTA", [12, 128, S])
    vA_d = dscr("s_vA", [16, 128, 1536])
    qTB_d = dscr("s_qTB", [4, 128, S])
    kTB_d = dscr("s_kTB", [4, 128, S])
    vB_d = dscr("s_vB", [16, 128, 512])
    gaT_d = dscr("s_gaT", [16, 128, S])
    gbT_d = dscr("s_gbT", [16, 128, S])
    aT_d = dscr("s_aT", [44, 128, S])
    if dbg:
        dbg_uT = dscr("s_uT", [16, 128, S])
        dbg_oT = dscr("s_oT", [8, 128, S])
        dbg_mod = dscr("s_mod", [128, 64 + 4096], F32)

    ARENA = 190 * KB
    arena = nc.alloc_sbuf_tensor("arena", [128, ARENA], U8)

    def carve(off, shape, dt):
        sz = mybir.dt.size(dt)
        n = _prod(shape[1:])
        assert off % 32 == 0 and off + n * sz <= ARENA, (off, shape)
        ap = arena.bitcast(dt)[:, off // sz: off // sz + n]
        if len(shape) == 3:
            ap = ap.rearrange("p (a b) -> p a b", a=shape[1])
        return ap

    R_CONST = 0
    R_MOD = 6 * KB
    R_BIG = 23 * KB
    R_WS = 87 * KB
    R_O = 135 * KB
    R_LOC = 167 * KB

    ident = carve(R_CONST, [128, 128], BF16)
    ones = carve(R_CONST + 256, [128, 128], BF16)
    band = carve(R_CONST + 512, [128, 256], BF16)
    ones_row = carve(R_CONST + 1024, [128, 2048], BF16)
    cact = carve(R_CONST + 5 * KB, [128, 16], BF16)
    epsT = carve(R_CONST + 5 * KB + 64, [128, 1], F32)
    modT = carve(R_MOD, [128, 64], F32)
    G1b = carve(R_MOD + 512, [128, 2048], F32)
    G2b = carve(R_MOD + 512 + 8 * KB, [128, 2048], F32)
    BIG = carve(R_BIG, [128, 16, 2048], BF16)
    WS = [carve(R_WS + i * 16 * KB, [128, 16, 512], BF16) for i in range(3)]

    banks_h = [nc.alloc_psum_tensor("bank%d" % i, [128, 512], F32) for i in range(8)]
    banks = [b[:, :] for b in banks_h]
    banks_bf = [b.bitcast(BF16)[:, :] for b in banks_h]
    Tb = [T("bank%d" % i) for i in range(8)]

    SCALE = float(HD ** -0.5)

    Tc = T("const")
    tmpf = carve(R_LOC, [128, 256], F32)
    P.emit("pool", lambda e: e.memset(tmpf[:, 0:128], 0.0), W=[Tc])
    P.emit("pool", lambda e: e.affine_select(out=tmpf[:, 0:128], in_=tmpf[:, 0:128], compare_op=ALU.not_equal,
                                             fill=1.0, base=0, pattern=[[-1, 128]], channel_multiplier=1),
           R=[Tc], W=[Tc])
    P.tcopy("dve", ident, tmpf[:, 0:128], [Tc], [Tc])
    P.emit("pool", lambda e: e.memset(ones, 1.0), R=[Tc], W=[Tc])
    P.emit("pool", lambda e: e.memset(ones_row, 1.0), R=[Tc], W=[Tc])
    P.emit("pool", lambda e: e.memset(epsT, EPS), R=[Tc], W=[Tc])
    P.emit("pool", lambda e: e.memset(tmpf, 1.0), R=[Tc], W=[Tc])
    P.emit("pool", lambda e: e.affine_select(out=tmpf, in_=tmpf, compare_op=ALU.is_ge, fill=0.0, base=0,
                                             pattern=[[1, 256]], channel_multiplier=-1), R=[Tc], W=[Tc])
    P.emit("pool", lambda e: e.affine_select(out=tmpf, in_=tmpf, compare_op=ALU.is_ge, fill=0.0, base=128,
                                             pattern=[[-1, 256]], channel_multiplier=1), R=[Tc], W=[Tc])
    P.tcopy("dve", band, tmpf, [Tc], [Tc])
    ctf = carve(R_LOC + 2 * KB, [128, 16], F32)
    P.dma("sp", ctf, ct_d, W=[Tc], semt=Tc)
    P.act(cact, ctf, AF.Silu, [Tc], [Tc])
    P.barrier()

    def stop(name):
        return stop_after == name

    def finish():
        P.barrier()
        P.build()
        return nc

    def phase_mod(l):
        Tm = T("mod")
        Crep = carve(R_BIG, [128, 16, 128], BF16)
        for kc in range(16):
            P.tcopy("dve", Crep[:, kc, :], cact[:, kc:kc + 1].to_broadcast([128, 128]), [], [Tm], accum=True)
        wv = wada_d[l].rearrange("(kc p) n -> p kc n", p=128)
        Tw = [T("w%d" % i) for i in range(3)]
        Tbb = [T("bb%d" % i) for i in range(2)]
        bbs = [carve(R_LOC + i * 2 * KB, [128, 512], F32) for i in range(2)]
        badT = carve(R_LOC + 4 * KB, [128, 96], F32)
        Tbad = T("badT")
        P.dma("sp", badT, badaT_d[l], W=[Tbad], semt=Tbad)
        n1 = carve(R_LOC + 5 * KB, [128, 16], F32)
        n2 = carve(R_LOC + 5 * KB + 64, [128, 16], F32)
        P.dma("sp", n1, n1_d[l], W=[Tbad], semt=Tbad, accum=True)
        P.dma("sp", n2, n2_d[l], W=[Tbad], semt=Tbad, accum=True)

        def load(ch):
            if ch < 24:
                P.dma("pool", WS[ch % 3], wv[:, :, ch * 512:(ch + 1) * 512], W=[Tw[ch % 3]], semt=Tw[ch % 3])
        load(0)
        load(1)
        mbank, Tmb = banks[7], Tb[7]
        nb = 0
        ng = 0
        for ch in range(24):
            load(ch + 2)
            w, tw = WS[ch % 3], Tw[ch % 3]
            v = ch // 4
            if v in (2, 5):
                bk, tbk = banks[nb % 2], Tb[nb % 2]
                nb += 1
                for kc in range(16):
                    P.mm(bk, Crep[:, kc, :], w[:, kc, :], kc == 0, kc == 15, [tw, Tm], [tbk])
                bb, tbb = bbs[ng % 2], Tbb[ng % 2]
                ng += 1
                P.dma("sp", bb, bada_d[l, ch * 512:(ch + 1) * 512].partition_broadcast(128), W=[tbb], semt=tbb)
                G = G1b if v == 2 else G2b
                c0 = (ch % 4) * 512
                P.tt("dve", G[:, c0:c0 + 512], bk, bb, ALU.add, [tbk, tbb], [Tm], accum=True)
            else:
                vp = {0: 0, 1: 1, 3: 2, 4: 3}[v]
                for cb in range(4):
                    col = vp * 16 + (ch % 4) * 4 + cb
                    for kc in range(16):
                        P.mm(mbank[:, col:col + 1], w[:, kc, cb * 128:(cb + 1) * 128], cact[:, kc:kc + 1],
                             kc == 0, kc == 15, [tw, Tc], [Tmb])
        for vp, v in enumerate((0, 1, 3, 4)):
            P.tt("dve", modT[:, vp * 16:(vp + 1) * 16], mbank[:, vp * 16:(vp + 1) * 16],
                 badT[:, v * 16:(v + 1) * 16], ALU.add, [Tmb, Tbad], [Tm], accum=True)
        P.stt(modT[:, 16:32], modT[:, 16:32], 1.0, n1, ALU.add, ALU.mult, [Tm, Tbad], [Tm])
        P.stt(modT[:, 48:64], modT[:, 48:64], 1.0, n2, ALU.add, ALU.mult, [Tm, Tbad], [Tm])
        if dbg and l == 0:
            P.dma("sp", dbg_mod[:, 0:64], modT, R=[Tm], semt=Tm)
            P.dma("sp", dbg_mod[:, 64:64 + 2048], G1b, R=[Tm], semt=Tm)
            P.dma("sp", dbg_mod[:, 64 + 2048:64 + 4096], G2b, R=[Tm], semt=Tm)
        P.barrier()

    def phase_norm(h_src, Scol, SHcol):
        hb = [carve(R_O + i * 8 * KB, [128, 2048], F32) for i in range(2)]
        Th = [T("h%d" % i) for i in range(2)]
        xh = [carve(R_O + 16 * KB + i * 4 * KB, [128, 2048], BF16) for i in range(4)]
        Tx = [T("xh%d" % i) for i in range(4)]
        junk = carve(R_LOC, [128, 2048], BF16)
        Tj = T("junk")
        st = [carve(R_LOC + 4 * KB + i * 32, [128, 4], F32) for i in range(4)]
        Ts = [T("st%d" % i) for i in range(4)]
        TuT = T("uT")
        trr = Ring(list(range(4)))

        def load(tt):
            if tt < NT:
                P.dma("sp", hb[tt % 2], h_src[tt * 128:(tt + 1) * 128, :], W=[Th[tt % 2]], semt=Th[tt % 2])
        load(0)
        for tq in range(4):
            for t4 in range(4):
                tt = tq * 4 + t4
                load(tt + 1)
                h, th = hb[tt % 2], Th[tt % 2]
                s, tsn = st[t4], Ts[t4]
                P.emit("pool", lambda e, s=s: e.memset(s, 0.0), W=[tsn])
                P.act(junk, h, AF.Square, [th, tsn], [Tj, tsn], accum_out=s[:, 0:1])
                P.act(s[:, 1:2], s[:, 0:1], AF.Sqrt, [tsn], [tsn], scale=1.0 / D, bias=epsT)
                P.emit("dve", lambda e, s=s: e.reciprocal(s[:, 2:3], s[:, 1:2]), R=[tsn], W=[tsn])
                P.ts("dve", xh[t4], h, s[:, 2:3], None, ALU.mult, None, [th, tsn], [Tx[t4]])
            for j in range(16):
                b = trr.next()
                for t4 in range(4):
                    P.tr(banks_bf[b][:, t4 * 128:(t4 + 1) * 128], xh[t4][:, j * 128:(j + 1) * 128], ident,
                         [Tx[t4], Tc], [Tb[b]])
                out = BIG[:, j, tq * 512:(tq + 1) * 512]
                if j % 2 == 0:
                    P.act(out, banks_bf[b][:, 0:512], AF.Identity, [Tb[b]], [TuT], scale=modT[:, Scol + j:Scol + j + 1],
                          bias=modT[:, SHcol + j:SHcol + j + 1], accum=True)
                else:
                    P.ts("dve", out, banks_bf[b][:, 0:512], modT[:, Scol + j:Scol + j + 1],
                         modT[:, SHcol + j:SHcol + j + 1], ALU.mult, ALU.add, [Tb[b]], [TuT], accum=True)
        P.barrier()

    def phase_inproj(l):
        TuT = T("uT")
        Tw = [T("w%d" % i) for i in range(3)]
        wv = win_d[l].rearrange("(kc p) n -> p kc n", p=128)
        tabs = [carve(R_O + i * 8 * KB, [128, 16, 128], F32) for i in range(4)]
        Ttab = T("tabs")
        gq = carve(R_LOC + 20 * KB + 512, [128, 128], F32)
        gk = carve(R_LOC + 21 * KB, [128, 128], F32)
        gqs = carve(R_LOC + 21 * KB + 512, [128, 128], F32)
        gks = carve(R_LOC + 22 * KB, [128, 128], F32)
        Tg = T("gains")
        P.dma("sp", gq, qn_d[l].partition_broadcast(128), W=[Tg], semt=Tg)
        P.dma("sp", gk, kn_d[l].partition_broadcast(128), W=[Tg], semt=Tg, accum=True)
        P.dma("sp", gqs[:, 0:64], qn_d[l, 64:128].partition_broadcast(128), W=[Tg], semt=Tg, accum=True)
        P.dma("sp", gqs[:, 64:128], qn_d[l, 0:64].partition_broadcast(128), W=[Tg], semt=Tg, accum=True)
        P.dma("sp", gks[:, 0:64], kn_d[l, 64:128].partition_broadcast(128), W=[Tg], semt=Tg, accum=True)
        P.dma("sp", gks[:, 64:128], kn_d[l, 0:64].partition_broadcast(128), W=[Tg], semt=Tg, accum=True)
        csv = cs_d.rearrange("(t p) c -> p t c", p=128)
        snv = sn_d.rearrange("(t p) c -> p t c", p=128)
        for i, (src, g) in enumerate(((csv, gq), (snv, gqs), (csv, gk), (snv, gks))):
            P.dma("sp", tabs[i], src, W=[Ttab], semt=Ttab, accum=True)
        for i, g in enumerate((gq, gqs, gk, gks)):
            P.tt("pool", tabs[i], tabs[i], g.unsqueeze(1).to_broadcast([128, 16, 128]), ALU.mult, [Ttab, Tg], [Ttab])

        X = [carve(R_LOC + i * 2 * KB, [128, 4, 128], F32) for i in range(2)]
        t1 = [carve(R_LOC + 4 * KB + i * 2 * KB, [128, 4, 128], F32) for i in range(2)]
        t2 = [carve(R_LOC + 8 * KB + i * 2 * KB, [128, 4, 128], F32) for i in range(2)]
        ob = [carve(R_LOC + 12 * KB + i * KB, [128, 4, 128], BF16) for i in range(3)]
        sT = [carve(R_LOC + 15 * KB + i * KB, [128, 4, 128], BF16) for i in range(2)]
        stg = [carve(R_LOC + 17 * KB + i * KB, [128, 512], BF16) for i in range(3)]
        rs = [carve(R_LOC + 20 * KB + i * 64, [128, 12], F32) for i in range(2)]
        TX = [T("X%d" % i) for i in range(2)]
        Tt1 = [T("t1%d" % i) for i in range(2)]
        Tt2 = [T("t2%d" % i) for i in range(2)]
        Tob = [T("ob%d" % i) for i in range(3)]
        TsT = [T("sT%d" % i) for i in range(2)]
        Tstg = [T("stg%d" % i) for i in range(3)]
        Trs = [T("rs%d" % i) for i in range(2)]
        mmr = Ring(list(range(6)))
        trr = Ring([6, 7])
        rp = [0]
        sg = [0]

        def load(cc):
            if cc < 20:
                P.dma("pool", WS[cc % 3], wv[:, :, cc * 512:(cc + 1) * 512], W=[Tw[cc % 3]], semt=Tw[cc % 3])

        pend = []

        def rope(b, tt, isk, hc, dst):
            n = rp[0]
            i = n % 2
            o = n % 3
            rp[0] += 1
            Xi, t1i, t2i, obi, rsi = X[i], t1[i], t2[i], ob[o], rs[i]
            T1 = tabs[2 * isk][:, tt, :]
            T2 = tabs[2 * isk + 1][:, tt, :]
            bk = banks[b].rearrange("p (a b) -> p a b", a=4)
            P.tcopy("act", Xi, bk, [Tb[b]], [TX[i]])
            P.act(t1i, Xi, AF.Square, [TX[i]], [Tt1[i]])
            P.emit("dve", lambda e: e.tensor_reduce(out=rsi[:, 0:4], in_=t1i, axis=AX.X, op=ALU.add),
                   R=[Tt1[i]], W=[Trs[i]])
            P.act(rsi[:, 4:8], rsi[:, 0:4], AF.Sqrt, [Trs[i]], [Trs[i]], scale=1.0 / HD, bias=epsT)
            P.emit("dve", lambda e: e.reciprocal(rsi[:, 8:12], rsi[:, 4:8]), R=[Trs[i]], W=[Trs[i]])
            P.tt("dve", t1i, Xi, T1.unsqueeze(1).to_broadcast([128, 4, 128]), ALU.mult, [TX[i], Ttab, Trs[i]], [Tt1[i]])
            P.tt("pool", t2i[:, :, 0:64], Xi[:, :, 64:128], T2[:, 0:64].unsqueeze(1).to_broadcast([128, 4, 64]),
                 ALU.mult, [TX[i], Ttab], [Tt2[i]])
            P.tt("pool", t2i[:, :, 64:128], Xi[:, :, 0:64], T2[:, 64:128].unsqueeze(1).to_broadcast([128, 4, 64]),
                 ALU.mult, [TX[i], Ttab], [Tt2[i]], accum=True)
            P.tt("pool", t1i, t1i, t2i, ALU.add, [Tt1[i], Tt2[i]], [Tt1[i]])
            P.tt("dve", obi, t1i, rsi[:, 8:12].unsqueeze(2).to_broadcast([128, 4, 128]), ALU.mult,
                 [Tt1[i], Trs[i]], [Tob[o]])
            pend.append((n, tt, hc, dst))
            if len(pend) > 2:
                rope2(*pend.pop(0))

        def rope2(n, tt, hc, dst):
            o = n % 3
            i = n % 2
            obi, sTi = ob[o], sT[i]
            tb = trr.next()
            for hh in range(4):
                P.tr(banks_bf[tb][:, hh * 128:(hh + 1) * 128], obi[:, hh, :], ident, [Tob[o], Tc], [Tb[tb]])
            P.evac(sTi, banks_bf[tb][:, 0:512].rearrange("p (a b) -> p a b", a=4), [Tb[tb]], [TsT[i]])
            P.dma("sp", dst[hc * 4:(hc + 1) * 4, :, tt * 128:(tt + 1) * 128].rearrange("h p t -> p h t"), sTi,
                  R=[TsT[i]], semt=TsT[i])

        load(0)
        load(1)
        for cc in range(20):
            load(cc + 2)
            w, tw = WS[cc % 3], Tw[cc % 3]
            if cc < 6:
                isk = cc // 3
                for tt in range(NT):
                    b = mmr.next()
                    for kc in range(16):
                        P.mm(banks[b], BIG[:, kc, tt * 128:(tt + 1) * 128], w[:, kc, :], kc == 0, kc == 15,
                             [tw, TuT], [Tb[b]])
                    rope(b, tt, isk, cc % 3, kTA_d if isk else qTA_d)
            elif cc < 9:
                while pend:
                    rope2(*pend.pop(0))
                g = cc - 6
                d = (1, 4, 16)[g]
                nbk = 16 // d
                for idx in range(16):
                    r, n = idx // nbk, idx % nbk
                    st0 = r + d * 128 * n
                    b = mmr.next()
                    for kc in range(16):
                        P.mm(banks[b], BIG[:, kc, st0:st0 + d * 127 + 1:d], w[:, kc, :], kc == 0, kc == 15,
                             [tw, TuT], [Tb[b]])
                    i = sg[0] % 3
                    sg[0] += 1
                    P.evac(stg[i], banks[b], [Tb[b]], [Tstg[i]])
                    P.dma("sp", vA_d[idx, :, g * 512:(g + 1) * 512], stg[i], R=[Tstg[i]], semt=Tstg[i])
            elif cc == 11:
                for tt in range(NT):
                    b = mmr.next()
                    for kc in range(16):
                        P.mm(banks[b], BIG[:, kc, tt * 128:(tt + 1) * 128], w[:, kc, :], kc == 0, kc == 15,
                             [tw, TuT], [Tb[b]])
                    i = sg[0] % 3
                    sg[0] += 1
                    P.evac(stg[i], banks[b], [Tb[b]], [Tstg[i]])
                    P.dma("sp", vB_d[tt], stg[i], R=[Tstg[i]], semt=Tstg[i])
            else:
                for cb in range(4):
                    if cc == 9:
                        dst, sig = qTB_d[cb], False
                    elif cc == 10:
                        dst, sig = kTB_d[cb], False
                    elif cc < 16:
                        dst, sig = gaT_d[(cc - 12) * 4 + cb], True
                    else:
                        dst, sig = gbT_d[(cc - 16) * 4 + cb], True
                    for tq in range(4):
                        b = mmr.next()
                        for kc in range(16):
                            P.mm(banks[b], w[:, kc, cb * 128:(cb + 1) * 128], BIG[:, kc, tq * 512:(tq + 1) * 512],
                                 kc == 0, kc == 15, [tw, TuT], [Tb[b]])
                        i = sg[0] % 3
                        sg[0] += 1
                        if sig:
                            P.act(stg[i], banks[b], AF.Sigmoid, [Tb[b]], [Tstg[i]])
                        else:
                            P.evac(stg[i], banks[b], [Tb[b]], [Tstg[i]])
                        P.dma("sp", dst[:, tq * 512:(tq + 1) * 512], stg[i], R=[Tstg[i]], semt=Tstg[i])
        P.barrier()

    OAT = carve(R_O, [128, 4, 2048], BF16)
    OBT = carve(R_O + 16 * KB, [128, 4, 2048], BF16)

    def phase_mixer_a():
        numacc = carve(R_BIG, [128, 4, 2048], F32)
        denacc = carve(R_BIG + 32 * KB, [128, 4, 2048], F32)
        Tacc = [T("acc%d" % j) for j in range(4)]
        Tdacc = [T("dacc%d" % j) for j in range(4)]
        for j in range(4):
            P.emit("pool", lambda e, j=j: e.memset(numacc[:, j, :], 0.0), W=[Tacc[j]])
            P.emit("pool", lambda e, j=j: e.memset(denacc[:, j, :], 0.0), W=[Tdacc[j]])
        Vb = [carve(R_WS + i * 16 * KB, [128, 16, 512], BF16) for i in range(2)]
        TV = [T("V%d" % i) for i in range(2)]
        QK = [carve(R_WS + 32 * KB + i * 4 * KB, [128, 2048], BF16) for i in range(4)]
        TQK = [T("QK%d" % i) for i in range(4)]
        pT = [carve(R_LOC + i * 512, [128, 256], BF16) for i in range(4)]
        TpT = [T("pT%d" % i) for i in range(4)]
        rec = carve(R_LOC + 4 * KB, [128, 2048], F32)
        Trec = T("rec")
        sring = Ring([0, 1, 2, 3])
        slots = [(4 + s // 4, 6 + s // 4, (s % 4) * 128) for s in range(8)]
        Tns = [T("ns%d" % s) for s in range(8)]
        Tds = [T("ds%d" % s) for s in range(8)]
        sc = [0]
        pc = [0]
        loads = []
        for g in range(3):
            for j in range(4):
                loads.append((g, j))

        def load(i):
            if i < len(loads):
                g, j = loads[i]
                head = g * 4 + j
                if j == 0:
                    P.dma("sp", Vb[g % 2], vA_d[:, :, g * 512:(g + 1) * 512].rearrange("t p c -> p t c"),
                          W=[TV[g % 2]], semt=TV[g % 2])
                q = (i % 2) * 2
                P.dma("sp", QK[q], qTA_d[head], W=[TQK[q]], semt=TQK[q])
                P.dma("sp", QK[q + 1], kTA_d[head], W=[TQK[q + 1]], semt=TQK[q + 1])
        load(0)
        units = []
        for i, (g, j) in enumerate(loads):
            d = (1, 4, 16)[g]
            nbk = 16 // d
            for r in range(d):
                for kt in range(nbk):
                    units.append((i, g, j, d, nbk, r, kt))
        state = {"cur": None}

        def stage1(u, pi):
            i, g, j, d, nbk, r, kt = u
            q = (i % 2) * 2
            QT, KT, tq_, tk_ = QK[q], QK[q + 1], TQK[q], TQK[q + 1]
            nq = 256 if kt < nbk - 1 else 128
            k0 = r + d * 128 * kt
            sb = sring.next()
            P.mm(banks[sb][:, 0:nq], KT[:, k0:k0 + d * 127 + 1:d], QT[:, k0:k0 + d * (nq - 1) + 1:d],
                 True, True, [tq_, tk_], [Tb[sb]])
            P.act(pT[pi][:, 0:nq], banks[sb][:, 0:nq], AF.Exp, [Tb[sb]], [TpT[pi]], scale=SCALE)
            P.tt("dve", pT[pi][:, 0:nq], pT[pi][:, 0:nq], band[:, 0:nq], ALU.mult, [TpT[pi], Tc], [TpT[pi]])

        def stage2(u, pi):
            i, g, j, d, nbk, r, kt = u
            V, tv = Vb[g % 2], TV[g % 2]
            nq = 256 if kt < nbk - 1 else 128
            k0 = r + d * 128 * kt
            Vl = V[:, r * nbk + kt, j * 128:(j + 1) * 128]
            if kt == 0:
                cur = sc[0] % 8
                sc[0] += 1
                first = True
            else:
                cur = state["cur"]
                first = False
            nbank, dbank, c0 = slots[cur]
            P.mm(banks[nbank][:, c0:c0 + 128], Vl, pT[pi][:, 0:128], first, True, [tv, TpT[pi]], [Tns[cur]])
            P.mm(banks[dbank][:, c0:c0 + 128], ones, pT[pi][:, 0:128], first, True, [Tc, TpT[pi]], [Tds[cur]])
            tok = slice(k0, k0 + d * 127 + 1, d)
            P.tt("dve", numacc[:, j, tok], numacc[:, j, tok], banks[nbank][:, c0:c0 + 128], ALU.add,
                 [Tns[cur], Tacc[j]], [Tacc[j]])
            P.tt("dve", denacc[:, j, tok], denacc[:, j, tok], banks[dbank][:, c0:c0 + 128], ALU.add,
                 [Tds[cur], Tdacc[j]], [Tdacc[j]])
            if nq == 256:
                nxt = sc[0] % 8
                sc[0] += 1
                nbank2, dbank2, c2 = slots[nxt]
                P.mm(banks[nbank2][:, c2:c2 + 128], Vl, pT[pi][:, 128:256], True, False,
                     [tv, TpT[pi]], [Tns[nxt]])
                P.mm(banks[dbank2][:, c2:c2 + 128], ones, pT[pi][:, 128:256], True, False,
                     [Tc, TpT[pi]], [Tds[nxt]])
                state["cur"] = nxt
            else:
                state["cur"] = None

        LAG = 0
        lastload = -1
        for n, u in enumerate(units):
            if u[0] != lastload:
                lastload = u[0]
                load(u[0] + 1)
            stage1(u, n % 4)
            if n >= LAG:
                stage2(units[n - LAG], (n - LAG) % 4)
        for n in range(max(0, len(units) - LAG), len(units)):
            stage2(units[n], n % 4)
        for j in range(4):
            P.emit("dve", lambda e, j=j: e.reciprocal(rec, denacc[:, j, :]), R=[Tdacc[j]], W=[Trec])
            P.tt("dve", OAT[:, j, :], numacc[:, j, :], rec, ALU.mult, [Tacc[j], Trec], [Tdacc[j]])
        if dbg:
            for j in range(4):
                P.dma("sp", dbg_oT[j], OAT[:, j, :], R=[Tdacc[j]], semt=Tdacc[j])
        P.barrier()

    def phase_mixer_b():
        fbuf = [carve(R_BIG + i * 8 * KB, [128, 2048], F32) for i in range(8)] + [carve(R_LOC, [128, 2048], F32)]
        fb = [[fbuf[3 * s + i] for i in range(3)] for s in range(3)]
        Tfb = [[T("fb%d_%d" % (s, i)) for i in range(3)] for s in range(3)]
        QKV = [[carve(R_WS + (s * 3 + i) * 4 * KB, [128, 2048], BF16) for i in range(3)] for s in range(2)]
        TQKV = [[T("qkv%d_%d" % (s, i)) for i in range(3)] for s in range(2)]
        Ab = [carve(R_WS + 24 * KB + i * 4 * KB, [128, 2048], BF16) for i in range(2)]
        TA = [T("A%d" % i) for i in range(2)]
        ATb = [carve(R_WS + 32 * KB + i * 4 * KB, [128, 16, 128], BF16) for i in range(2)]
        TAT = [T("AT%d" % i) for i in range(2)]
        TOB = T("OBT")
        zr = Ring([0, 1, 2, 3])
        trr = Ring([4, 5])
        orr = Ring([6, 7])

        def load(hh):
            if hh < 4:
                s = hh % 2
                P.dma("sp", QKV[s][0], qTB_d[hh], W=[TQKV[s][0]], semt=TQKV[s][0])
                P.dma("sp", QKV[s][1], kTB_d[hh], W=[TQKV[s][1]], semt=TQKV[s][1])
                P.dma("sp", QKV[s][2].rearrange("p (t c) -> p t c", t=16),
                      vB_d[:, :, hh * 128:(hh + 1) * 128].rearrange("t p c -> p t c"),
                      W=[TQKV[s][2]], semt=TQKV[s][2])
        load(0)

        def stA(hh, qt, bs):
            s = hh % 2
            QT, KT = QKV[s][0], QKV[s][1]
            tq_, tk_, tv_ = TQKV[s]
            nk = 128 * (qt + 1)
            SPt, LB, G = fb[bs]
            tSP, tLB, tG = Tfb[bs]
            nch = (nk + 511) // 512
            for c in range(nch):
                w = min(512, nk - c * 512)
                cs_ = slice(c * 512, c * 512 + w)
                zb = zr.next()
                P.mm(banks[zb][:, 0:w], QT[:, qt * 128:(qt + 1) * 128], KT[:, cs_], True, True,
                     [tq_, tk_], [Tb[zb]])
                P.act(SPt[:, cs_], banks[zb][:, 0:w], AF.Exp, [Tb[zb]], [tSP], scale=SCALE, accum=(c > 0))
                P.act(SPt[:, cs_], SPt[:, cs_], AF.Ln, [tSP], [tSP], bias=1.0, accum=True)
                P.stt(LB[:, cs_], banks[zb][:, 0:w], SCALE, SPt[:, cs_], ALU.mult, ALU.subtract,
                      [Tb[zb], tSP], [tLB], accum=(c > 0))
            dg = slice(nk - 128, nk)
            P.emit("pool", lambda e: e.affine_select(
                out=SPt[:, dg], in_=SPt[:, dg], compare_op=ALU.is_ge, fill=0.0, base=-1,
                pattern=[[-1, 128]], channel_multiplier=1), R=[tSP, tLB], W=[tSP])

        def stB_dve(hh, qt, bs):
            nk = 128 * (qt + 1)
            SPt, LB, G = fb[bs]
            tSP, tLB, tG = Tfb[bs]
            P.emit("dve", lambda e: e.tensor_tensor_scan(
                out=G[:, 0:nk], data0=ones_row[:, 0:nk], data1=SPt[:, 0:nk], initial=0.0,
                op0=ALU.mult, op1=ALU.add), R=[tSP, Tc], W=[tG])
            P.stt(LB[:, 0:nk], G[:, 0:nk], G[:, nk - 1:nk], LB[:, 0:nk], ALU.subtract, ALU.add,
                  [tG, tLB], [tLB])

        def stB_act(hh, qt, bs, a):
            nk = 128 * (qt + 1)
            SPt, LB, G = fb[bs]
            tSP, tLB, tG = Tfb[bs]
            A, tA = Ab[a], TA[a]
            dg = slice(nk - 128, nk)
            P.act(A[:, 0:nk], LB[:, 0:nk], AF.Exp, [tLB], [tA])
            P.emit("pool", lambda e: e.affine_select(
                out=A[:, dg], in_=A[:, dg], compare_op=ALU.is_ge, fill=0.0, base=-1,
                pattern=[[-1, 128]], channel_multiplier=1), R=[tA], W=[tA])

        def stC(hh, qt, a):
            s = hh % 2
            V = QKV[s][2].rearrange("p (t c) -> p t c", t=16)
            tq_, tk_, tv_ = TQKV[s]
            A, tA = Ab[a], TA[a]
            AT, tAT = ATb[a], TAT[a]
            nblk = qt + 1
            for g0 in range(0, nblk, 4):
                gn = min(4, nblk - g0)
                tb = trr.next()
                for i in range(gn):
                    kb = g0 + i
                    P.tr(banks_bf[tb][:, i * 128:(i + 1) * 128], A[:, kb * 128:(kb + 1) * 128], ident,
                         [tA, Tc], [Tb[tb]])
                P.evac(AT[:, g0:g0 + gn, :], banks_bf[tb][:, 0:gn * 128].rearrange("p (a b) -> p a b", a=gn),
                       [Tb[tb]], [tAT], accum=(g0 > 0))
            ob_ = orr.next()
            for kb in range(nblk):
                P.mm(banks[ob_][:, 0:128], V[:, kb, :], AT[:, kb, :], kb == 0, kb == nblk - 1,
                     [tv_, tAT], [Tb[ob_]])
            P.evac(OBT[:, hh, qt * 128:(qt + 1) * 128], banks[ob_][:, 0:128], [Tb[ob_]], [TOB], accum=True)

        units = [(hh, qt) for hh in range(4) for qt in range(NT)]
        N = len(units)
        for n in range(N + 2):
            if 0 <= n - 1 < N:
                stB_dve(units[n - 1][0], units[n - 1][1], (n - 1) % 3)
            if n < N:
                stA(units[n][0], units[n][1], n % 3)
            if 0 <= n - 1 < N:
                stB_act(units[n - 1][0], units[n - 1][1], (n - 1) % 3, (n - 1) % 2)
            if 0 <= n - 2 < N:
                hh, qt = units[n - 2]
                stC(hh, qt, (n - 2) % 2)
                if qt == 0:
                    load(hh + 1)
        if dbg:
            for j in range(4):
                P.dma("sp", dbg_oT[4 + j], OBT[:, j, :], R=[TOB], semt=TOB)
        P.barrier()

    def phase_merge(l):
        MT = BIG
        TMT = T("MT")
        wba = carve(R_WS, [128, 4, 2048], BF16)
        wbb = carve(R_WS + 16 * KB, [128, 4, 2048], BF16)
        Twb = T("wb")
        To = T("o")
        P.dma("pool", wba, wba_d[l].rearrange("(k p) n -> p k n", p=128), W=[Twb], semt=Twb)
        P.dma("pool", wbb, wbb_d[l].rearrange("(k p) n -> p k n", p=128), W=[Twb], semt=Twb, accum=True)
        sg = [carve(R_WS + 32 * KB + i * 4 * KB, [128, 2048], BF16) for i in range(4)]
        Tsg = [T("sg%d" % i) for i in range(4)]
        m1 = [carve(R_LOC + i * 2 * KB, [128, 512], F32) for i in range(2)]
        m2 = [carve(R_LOC + 4 * KB + i * 2 * KB, [128, 512], F32) for i in range(2)]
        Tm1 = [T("m1%d" % i) for i in range(2)]
        Tm2 = [T("m2%d" % i) for i in range(2)]
        br = Ring([0, 1, 2, 3, 4, 5, 6, 7])

        def load(fc):
            if fc < 16:
                s = (fc % 2) * 2
                P.dma("sp", sg[s], gaT_d[fc], W=[Tsg[s]], semt=Tsg[s])
                P.dma("sp", sg[s + 1], gbT_d[fc], W=[Tsg[s + 1]], semt=Tsg[s + 1])
        load(0)
        it = 0
        for fc in range(16):
            load(fc + 1)
            s = (fc % 2) * 2
            for tq in range(4):
                ts_ = slice(tq * 512, (tq + 1) * 512)
                ya = br.next()
                for j in range(4):
                    P.mm(banks[ya], wba[:, j, fc * 128:(fc + 1) * 128], OAT[:, j, ts_], j == 0, j == 3, [Twb, To], [Tb[ya]])
                yb = br.next()
                for j in range(4):
                    P.mm(banks[yb], wbb[:, j, fc * 128:(fc + 1) * 128], OBT[:, j, ts_], j == 0, j == 3, [Twb, To], [Tb[yb]])
                i = it % 2
                it += 1
                P.tt("dve", m1[i], banks[ya], sg[s][:, ts_], ALU.mult, [Tb[ya], Tsg[s]], [Tm1[i]])
                P.tt("dve", m2[i], banks[yb], sg[s + 1][:, ts_], ALU.mult, [Tb[yb], Tsg[s + 1]], [Tm2[i]])
                P.tt("pool", MT[:, fc, ts_], m1[i], m2[i], ALU.add, [Tm1[i], Tm2[i]], [TMT], accum=True)
        P.barrier()

    def resid_update(bank, tbank, Gb, h_src, tt, c, hp, thp, tmp, ttmp):
        rows = slice(tt * 128, (tt + 1) * 128)
        cols = slice(c * 512, (c + 1) * 512)
        P.dma("sp", hp, h_src[rows, cols], W=[thp], semt=thp)
        P.tt("dve", tmp, bank, Gb[:, cols], ALU.mult, [tbank], [ttmp])
        P.tt("pool", hp, hp, tmp, ALU.add, [thp, ttmp], [thp])
        P.dma("sp", y_d[rows, cols], hp, R=[thp], semt=thp)

    def phase_outproj(l, h_src):
        MT = BIG
        TMT = T("MT")
        Tw = [T("w%d" % i) for i in range(3)]
        wv = wout_d[l].rearrange("(kc p) n -> p kc n", p=128)
        hp = [carve(R_LOC + i * 2 * KB, [128, 512], F32) for i in range(4)]
        Thp = [T("hp%d" % i) for i in range(4)]
        tmp = [carve(R_LOC + 8 * KB + i * 2 * KB, [128, 512], F32) for i in range(2)]
        Ttmp = [T("tmp%d" % i) for i in range(2)]
        br = Ring([0, 1, 2, 3, 4, 5, 6, 7])

        def load(c):
            if c < 4:
                P.dma("pool", WS[c % 3], wv[:, :, c * 512:(c + 1) * 512], W=[Tw[c % 3]], semt=Tw[c % 3])
        load(0)
        load(1)
        it = 0
        for c in range(4):
            load(c + 2)
            w, tw = WS[c % 3], Tw[c % 3]
            for tt in range(NT):
                b = br.next()
                for fc in range(16):
                    P.mm(banks[b], MT[:, fc, tt * 128:(tt + 1) * 128], w[:, fc, :], fc == 0, fc == 15, [tw, TMT], [Tb[b]])
                resid_update(banks[b], Tb[b], G1b, h_src, tt, c, hp[it % 4], Thp[it % 4], tmp[it % 2], Ttmp[it % 2])
                it += 1
        P.barrier()

    def phase_gateup(l):
        TuT = T("uT")
        Tw = [T("w%d" % i) for i in range(3)]
        wv = wgu_d[l].rearrange("(kc p) n -> p kc n", p=128)
        sgl = [carve(R_LOC + i * 2 * KB, [128, 512], F32) for i in range(2)]
        Tsgl = [T("sgl%d" % i) for i in range(2)]
        ast = [carve(R_LOC + 4 * KB + i * KB, [128, 512], BF16) for i in range(3)]
        Tast = [T("ast%d" % i) for i in range(3)]
        br = Ring([0, 1, 2, 3, 4, 5, 6, 7])

        def load(st):
            if st < 22:
                i = st % 3
                P.dma("pool", WS[i][:, :, 0:256], wv[:, :, st * 256:(st + 1) * 256], W=[Tw[i]], semt=Tw[i])
                P.dma("pool", WS[i][:, :, 256:512], wv[:, :, DFF + st * 256:DFF + (st + 1) * 256], W=[Tw[i]],
                      semt=Tw[i], accum=True)
        load(0)
        load(1)
        it = 0
        for st in range(22):
            load(st + 2)
            w, tw = WS[st % 3], Tw[st % 3]
            for fb2 in range(2):
                fbi = st * 2 + fb2
                for tq in range(4):
                    ts_ = slice(tq * 512, (tq + 1) * 512)
                    gb_ = br.next()
                    for kc in range(16):
                        P.mm(banks[gb_], w[:, kc, fb2 * 128:(fb2 + 1) * 128], BIG[:, kc, ts_], kc == 0, kc == 15,
                             [tw, TuT], [Tb[gb_]])
                    ub_ = br.next()
                    for kc in range(16):
                        P.mm(banks[ub_], w[:, kc, 256 + fb2 * 128:256 + (fb2 + 1) * 128], BIG[:, kc, ts_], kc == 0,
                             kc == 15, [tw, TuT], [Tb[ub_]])
                    i = it % 2
                    k = it % 3
                    it += 1
                    P.act(sgl[i], banks[gb_], AF.Silu, [Tb[gb_]], [Tsgl[i]])
                    P.tt("dve", ast[k], sgl[i], banks[ub_], ALU.mult, [Tsgl[i], Tb[ub_]], [Tast[k]])
                    P.dma("sp", aT_d[fbi, :, ts_], ast[k], R=[Tast[k]], semt=Tast[k])
        P.barrier()

    def phase_down(l):
        Wd = [carve(R_BIG, [128, 44, 512], BF16), carve(R_WS, [128, 44, 512], BF16)]
        TWd = [T("Wd0"), T("Wd1")]
        wv = wdn_d[l].rearrange("(fb p) n -> p fb n", p=128)
        at = [carve(R_O, [128, 11, 512], BF16), carve(R_O + 11 * KB, [128, 11, 512], BF16),
              carve(R_BIG + 44 * KB, [128, 11, 512], BF16)]
        Tat = [T("at%d" % i) for i in range(3)]
        hp = [carve(R_LOC + i * 2 * KB, [128, 512], F32) for i in range(4)]
        Thp = [T("hp%d" % i) for i in range(4)]
        tmp = [carve(R_LOC + 8 * KB + i * 2 * KB, [128, 512], F32) for i in range(2)]
        Ttmp = [T("tmp%d" % i) for i in range(2)]
        atl = [(c, tq, kg) for c in range(4) for tq in range(4) for kg in range(4)]

        def loadw(c):
            if c < 4:
                for kg in range(4):
                    P.dma("pool", Wd[c % 2][:, kg * 11:(kg + 1) * 11, :], wv[:, kg * 11:(kg + 1) * 11, c * 512:(c + 1) * 512],
                          W=[TWd[c % 2]], semt=TWd[c % 2], accum=(kg > 0))

        def loada(i):
            if i < len(atl):
                c, tq, kg = atl[i]
                P.dma("sp", at[i % 3], aT_d[kg * 11:(kg + 1) * 11, :, tq * 512:(tq + 1) * 512].rearrange("f p t -> p f t"),
                      W=[Tat[i % 3]], semt=Tat[i % 3])
        loadw(0)
        loada(0)
        loada(1)
        ai = 0
        it = 0
        grp = 0
        for c in range(4):
            loadw(c + 1)
            W_, tW = Wd[c % 2], TWd[c % 2]
            for tq in range(4):
                bset = [0, 1, 2, 3] if grp % 2 == 0 else [4, 5, 6, 7]
                grp += 1
                for kg in range(4):
                    loada(ai + 2)
                    a_, ta = at[ai % 3], Tat[ai % 3]
                    ai += 1
                    for t4 in range(4):
                        b = bset[t4]
                        for f in range(11):
                            P.mm(banks[b], a_[:, f, t4 * 128:(t4 + 1) * 128], W_[:, kg * 11 + f, :],
                                 kg == 0 and f == 0, kg == 3 and f == 10, [ta, tW], [Tb[b]])
                for t4 in range(4):
                    b = bset[t4]
                    resid_update(banks[b], Tb[b], G2b, y_d, tq * 4 + t4, c, hp[it % 4], Thp[it % 4], tmp[it % 2], Ttmp[it % 2])
                    it += 1
        P.barrier()

    for l in range(nlayers):
        phase_mod(l)
        if stop("mod"):
            return finish()
        phase_norm(x_d if l == 0 else y_d, 16, 0)
        if dbg and l == 0:
            Td = T("dbg")
            for j in range(16):
                P.dma("sp", dbg_uT[j], BIG[:, j, :], semt=Td)
            P.barrier()
        if stop("norm"):
            return finish()
        phase_inproj(l)
        if stop("inproj"):
            return finish()
        phase_mixer_a()
        if stop("mixa"):
            return finish()
        phase_mixer_b()
        if stop("mixb"):
            return finish()
        phase_merge(l)
        phase_outproj(l, x_d if l == 0 else y_d)
        if stop("outproj"):
            return finish()
        phase_norm(y_d, 48, 32)
        phase_gateup(l)
        phase_down(l)
    return finish()


def host_inputs(inputs):
    f = lambda a: np.ascontiguousarray(np.asarray(a, dtype=np.float32))
    x = f(inputs["x"])
    c = f(inputs["c"])
    B = x.shape[0]
    inv = np.power(np.float32(10000.0), -np.arange(0, HD, 2, dtype=np.float32) / np.float32(HD)).astype(np.float32)
    ang = (np.arange(S, dtype=np.float32)[:, None] * inv[None, :]).astype(np.float32)
    cos = np.cos(ang.astype(np.float64)).astype(np.float32)
    sin = np.sin(ang.astype(np.float64)).astype(np.float32)
    shared = {
        "w_ada": f(inputs["w_ada"]),
        "b_ada": f(inputs["b_ada"]),
        "b_adaT": np.ascontiguousarray(f(inputs["b_ada"]).reshape(NL, 96, 128).transpose(0, 2, 1)),
        "n1T": np.ascontiguousarray(f(inputs["norm1_g"]).reshape(NL, 16, 128).transpose(0, 2, 1)),
        "n2T": np.ascontiguousarray(f(inputs["norm2_g"]).reshape(NL, 16, 128).transpose(0, 2, 1)),
        "w_in": f(inputs["w_in"]),
        "qn_g": f(inputs["qn_g"]),
        "kn_g": f(inputs["kn_g"]),
        "w_branch_a": f(inputs["w_branch_a"]),
        "w_branch_b": f(inputs["w_branch_b"]),
        "w_out": f(inputs["w_out"]),
        "w_gate_up": f(inputs["w_gate_up"]),
        "w_down": f(inputs["w_down"]),
        "rope_cs": np.ascontiguousarray(np.concatenate([cos, cos], axis=1)),
        "rope_sn": np.ascontiguousarray(np.concatenate([-sin, sin], axis=1)),
    }
    in_maps = []
    for b in range(B):
        m = dict(shared)
        m["x"] = x[b]
        m["ct"] = np.ascontiguousarray(c[b].reshape(16, 128).T)
        in_maps.append(m)
    return in_maps


def kernel(**inputs):
    in_maps = host_inputs(inputs)
    nc = build_program()
    res = run_bass_kernel_spmd(nc, in_maps, core_ids=list(range(len(in_maps))))
    return np.stack([np.asarray(r["y"], dtype=np.float32) for r in res.results], axis=0)
```
